# Optimizing a Trainium2 kernel written in Bass

```python
import math
import jax, jax.numpy as jnp
from jax import lax
import numpy as np

D_MODEL = 4096
BATCH = 4
SEQ = 2048
DEPTH = 2
DEC_BATCH = 8
DEC_SEQ = 1
PAST_LEN = 16384
PAGE_SIZE = 128

HEAD_DIM = 128
ATT_W = D_MODEL // 2
N_HEADS = ATT_W // HEAD_DIM
CONV_C = D_MODEL - ATT_W
CONV_K = 31
PLE_DIM = 256
DILATED_PAIRS = ((128, 1), (512, 4), (2048, 16))
WIN_MAX = max(w for w, _ in DILATED_PAIRS)
BLOCK = 128
N_BUCKETS = 32
MAX_DISTANCE = 2048
EPS = 1e-6
NEG = -1e30
SPLIT_SIZES = (ATT_W, ATT_W, ATT_W, ATT_W, CONV_C, CONV_C, CONV_C)
SPLIT_IDX = [int(s) for s in np.cumsum(SPLIT_SIZES)[:-1]]
IN_COLS = int(sum(SPLIT_SIZES))

kernel_name = "hymba_dilated_conformer_decoder_step"


def rmsnorm(x, g):
    xf = x.astype(jnp.float32)
    r = xf * lax.rsqrt(jnp.mean(xf * xf, axis=-1, keepdims=True) + EPS)
    return (r * g.astype(jnp.float32)).astype(x.dtype)


def layernorm(x, g, b):
    xf = x.astype(jnp.float32)
    mu = jnp.mean(xf, axis=-1, keepdims=True)
    var = jnp.mean(jnp.square(xf - mu), axis=-1, keepdims=True)
    y = (xf - mu) * lax.rsqrt(var + EPS) * g.astype(jnp.float32) + b.astype(jnp.float32)
    return y.astype(x.dtype)


def rel_bucket(dist):
    max_exact = N_BUCKETS // 2
    df = jnp.maximum(dist, 1).astype(jnp.float32)
    large = max_exact + (jnp.log(df / max_exact) / math.log(MAX_DISTANCE / max_exact)
                         * (N_BUCKETS - max_exact)).astype(jnp.int32)
    large = jnp.minimum(large, N_BUCKETS - 1)
    return jnp.where(dist < max_exact, dist, large)


def dilated_band_prompt(q, k, v, rel_bias, dil, band):
    B, S, H, E = q.shape
    L = S // dil
    Lp = -(-L // BLOCK) * BLOCK
    nb = Lp // BLOCK

    def to_sub(x):
        x = x.reshape(B, L, dil, H, E).transpose(0, 2, 1, 3, 4)
        return jnp.pad(x, ((0, 0), (0, 0), (0, Lp - L), (0, 0), (0, 0)))

    qs, ks, vs = to_sub(q), to_sub(k), to_sub(v)
    qb = qs.reshape(B, dil, nb, BLOCK, H, E)

    def band_keys(x):
        xp = jnp.pad(x, ((0, 0), (0, 0), (BLOCK, 0), (0, 0), (0, 0)))
        prev = xp[:, :, :Lp].reshape(B, dil, nb, BLOCK, H, E)
        cur = x.reshape(B, dil, nb, BLOCK, H, E)
        return jnp.concatenate([prev, cur], axis=3)

    kb, vb = band_keys(ks), band_keys(vs)
    qi = jnp.arange(BLOCK)[:, None]
    kj = jnp.arange(2 * BLOCK)[None, :]
    delta = qi + BLOCK - kj
    in_band = (delta >= 0) & (delta <= band)
    key_ok = (jnp.arange(nb)[:, None] * BLOCK + kj - BLOCK) >= 0
    mask = in_band[None] & key_ok[:, None, :]
    bias = rel_bias[rel_bucket(jnp.maximum(delta, 0) * dil)].astype(jnp.float32)
    bias = bias.transpose(2, 0, 1)

    s = jnp.einsum('bdnqhe,bdnkhe->bdnhqk', qb, kb).astype(jnp.float32) + bias
    s = jnp.where(mask[None, None, :, None], s, NEG)
    lse = jax.nn.logsumexp(s, axis=-1)
    pr = jnp.exp(s - lse[..., None]).astype(v.dtype)
    o = jnp.einsum('bdnhqk,bdnkhe->bdnqhe', pr, vb)
    o = o.reshape(B, dil, Lp, H, E)[:, :, :L].transpose(0, 2, 1, 3, 4).reshape(B, S, H, E)
    lse = lse.transpose(0, 1, 2, 4, 3).reshape(B, dil, Lp, H)[:, :, :L]
    lse = lse.transpose(0, 2, 1, 3).reshape(B, S, H)
    return o, lse


def dilated_gather_sample(q, kcat, vcat, rel_bias, dil, band, n_past):
    T = q.shape[1]
    i = jnp.arange(T)[:, None]
    j = jnp.arange(band + 1)[None, :]
    idx = n_past + i - j * dil
    valid = idx >= 0
    idxc = jnp.maximum(idx, 0)
    kg = kcat[:, idxc]
    vg = vcat[:, idxc]
    bias = rel_bias[rel_bucket(j * dil)].astype(jnp.float32)
    s = jnp.einsum('bthe,btjhe->bthj', q, kg).astype(jnp.float32) + bias.transpose(0, 2, 1)[None]
    s = jnp.where(valid[None, :, None, :], s, NEG)
    lse = jax.nn.logsumexp(s, axis=-1)
    pr = jnp.exp(s - lse[..., None]).astype(vcat.dtype)
    o = jnp.einsum('bthj,btjhe->bthe', pr, vg)
    return o, lse


def merge_by_denominator(outs):
    lse = jnp.stack([l for _, l in outs], axis=0)
    w = jax.nn.softmax(lse, axis=0)
    o = jnp.stack([o.astype(jnp.float32) for o, _ in outs], axis=0)
    return jnp.sum(w[..., None] * o, axis=0)


def attend_prompt(q, k, v, rel_bias):
    outs = [dilated_band_prompt(q, k, v, rel_bias, d, w // d) for (w, d) in DILATED_PAIRS]
    o = merge_by_denominator(outs).astype(v.dtype)
    keep = min(WIN_MAX, q.shape[1])
    return o, k[:, -keep:], v[:, -keep:]


def attend_sample(q, k, v, ck, cv, rel_bias):
    n_past = ck.shape[1]
    kcat = jnp.concatenate([ck.astype(k.dtype), k], axis=1)
    vcat = jnp.concatenate([cv.astype(v.dtype), v], axis=1)
    outs = [dilated_gather_sample(q, kcat, vcat, rel_bias, d, w // d, n_past) for (w, d) in DILATED_PAIRS]
    o = merge_by_denominator(outs).astype(v.dtype)
    keep = min(WIN_MAX, kcat.shape[1])
    return o, kcat[:, -keep:], vcat[:, -keep:]


def conv_module(glu_a, glu_b, gate_b, left, conv_w, conv_b, ln_g, ln_b):
    u = glu_a * jax.nn.sigmoid(glu_b)
    ucat = jnp.concatenate([left.astype(u.dtype), u], axis=1)
    y = lax.conv_general_dilated(ucat, conv_w.astype(u.dtype)[:, None, :], window_strides=(1,),
                                 padding='VALID', dimension_numbers=('NWC', 'WIO', 'NWC'),
                                 feature_group_count=u.shape[-1]) + conv_b
    y = jax.nn.silu(layernorm(y, ln_g, ln_b)) * jax.nn.silu(gate_b)
    return y, ucat[:, -(CONV_K - 1):]


def decoder_layer(h, p_l, attn_fn, conv_left, g_pre, w_in, conv_w, conv_b, ln_g, ln_b,
                  w_out, g_post, w_ple, g_ple, w_pg, b_pg):
    B, T, _ = h.shape
    xn = rmsnorm(h, g_pre)
    u = jnp.einsum('btd,dc->btc', xn, w_in)
    q, k, v, gate_a, glu_a, glu_b, gate_b = jnp.split(u, SPLIT_IDX, axis=-1)
    q = q.reshape(B, T, N_HEADS, HEAD_DIM) * (HEAD_DIM ** -0.5)
    k = k.reshape(B, T, N_HEADS, HEAD_DIM)
    v = v.reshape(B, T, N_HEADS, HEAD_DIM)
    o, k_state, v_state = attn_fn(q, k, v)
    y_a = o.reshape(B, T, ATT_W) * jax.nn.silu(gate_a)
    y_b, conv_state = conv_module(glu_a, glu_b, gate_b, conv_left, conv_w, conv_b, ln_g, ln_b)
    y = jnp.einsum('btc,cd->btd', jnp.concatenate([y_a, y_b], axis=-1), w_out)
    h = h + rmsnorm(y, g_post)
    gate = jax.nn.sigmoid(jnp.einsum('btd,de->bte', h, w_pg) + b_pg)
    h = h + gate * rmsnorm(jnp.einsum('btp,pd->btd', p_l, w_ple), g_ple)
    return h, k_state, v_state, conv_state


def setup_inputs(seed: int = 0) -> dict:
    key = jax.random.key(seed)
    ks = jax.random.split(key, 24)
    f32 = jnp.float32
    nrm = lambda k, shape, s: jax.random.normal(k, shape, f32) * s
    lw = min(WIN_MAX, PAST_LEN)
    return {
        "x_prompt": nrm(ks[0], (BATCH, SEQ, D_MODEL), 1.0),
        "x_sample": nrm(ks[1], (DEC_BATCH, DEC_SEQ, D_MODEL), 1.0),
        "cache_k": nrm(ks[2], (DEPTH, DEC_BATCH, lw, N_HEADS, HEAD_DIM), 1.0),
        "cache_v": nrm(ks[3], (DEPTH, DEC_BATCH, lw, N_HEADS, HEAD_DIM), 1.0),
        "state_conv": nrm(ks[4], (DEPTH, DEC_BATCH, CONV_K - 1, CONV_C), 0.5),
        "p_prompt": nrm(ks[5], (DEPTH, BATCH, SEQ, PLE_DIM), 1.0),
        "p_sample": nrm(ks[6], (DEPTH, DEC_BATCH, DEC_SEQ, PLE_DIM), 1.0),
        "rel_bias": nrm(ks[7], (N_BUCKETS, N_HEADS), 0.5),
        "g_pre": 1.0 + nrm(ks[8], (DEPTH, D_MODEL), 0.02),
        "w_in": nrm(ks[9], (DEPTH, D_MODEL, IN_COLS), D_MODEL ** -0.5),
        "conv_w": nrm(ks[10], (DEPTH, CONV_K, CONV_C), CONV_K ** -0.5),
        "conv_b": nrm(ks[11], (DEPTH, CONV_C), 0.01),
        "ln_g": 1.0 + nrm(ks[12], (DEPTH, CONV_C), 0.02),
        "ln_b": nrm(ks[13], (DEPTH, CONV_C), 0.01),
        "w_out": nrm(ks[14], (DEPTH, ATT_W + CONV_C, D_MODEL), (ATT_W + CONV_C) ** -0.5),
        "g_post": 1.0 + nrm(ks[15], (DEPTH, D_MODEL), 0.02),
        "w_ple": nrm(ks[16], (DEPTH, PLE_DIM, D_MODEL), PLE_DIM ** -0.5),
        "g_ple": 1.0 + nrm(ks[17], (DEPTH, D_MODEL), 0.02),
        "w_pg": nrm(ks[18], (DEPTH, D_MODEL, D_MODEL), D_MODEL ** -0.5),
        "b_pg": nrm(ks[19], (DEPTH, D_MODEL), 0.01),
    }


def reference(x_prompt, x_sample, cache_k, cache_v, state_conv, p_prompt, p_sample, rel_bias,
              g_pre, w_in, conv_w, conv_b, ln_g, ln_b, w_out, g_post, w_ple, g_ple, w_pg, b_pg):
    hp, hs = x_prompt, x_sample
    kp_l, vp_l, cp_l, ks_l, vs_l, cs_l = [], [], [], [], [], []
    zero_left = jnp.zeros((x_prompt.shape[0], CONV_K - 1, CONV_C), x_prompt.dtype)
    for l in range(DEPTH):
        wl = (g_pre[l], w_in[l], conv_w[l], conv_b[l], ln_g[l], ln_b[l],
              w_out[l], g_post[l], w_ple[l], g_ple[l], w_pg[l], b_pg[l])
        ap = lambda q, k, v: attend_prompt(q, k, v, rel_bias)
        hp, kp, vp, cp = decoder_layer(hp, p_prompt[l], ap, zero_left, *wl)
        ck, cv = cache_k[l], cache_v[l]
        asmp = lambda q, k, v, ck=ck, cv=cv: attend_sample(q, k, v, ck, cv, rel_bias)
        hs, ksn, vsn, csn = decoder_layer(hs, p_sample[l], asmp, state_conv[l], *wl)
        kp_l.append(kp); vp_l.append(vp); cp_l.append(cp)
        ks_l.append(ksn); vs_l.append(vsn); cs_l.append(csn)
    new_k_prompt = jnp.stack(kp_l, axis=0)
    new_v_prompt = jnp.stack(vp_l, axis=0)
    new_conv_prompt = jnp.stack(cp_l, axis=0)
    new_k_sample = jnp.stack(ks_l, axis=0)
    new_v_sample = jnp.stack(vs_l, axis=0)
    new_conv_sample = jnp.stack(cs_l, axis=0)
    return (hp, hs, new_k_prompt, new_v_prompt, new_conv_prompt, new_k_sample, new_v_sample, new_conv_sample)
```

```python
import math
from contextlib import ExitStack
import numpy as np
import concourse.bass as bass
import concourse.mybir as mybir
from concourse.bass_utils import run_bass_kernel_spmd

F32 = mybir.dt.float32
BF16 = mybir.dt.bfloat16
AF = mybir.ActivationFunctionType
ALU = mybir.AluOpType
AX = mybir.AxisListType

D = 4096
S = 2048
NT = S // 128
DEPTH = 2
H = 16
HD = 128
ATT = 2048
CC = 2048
INC = 14336
KC = D // 128
PLE = 256
CK = 31
LW = 2048
EPS = 1e-6
PW = 256
DILS = (1, 4, 16)
TL = 383
NB = 32


def _bucket(dist):
    max_exact = NB // 2
    df = np.maximum(dist, 1).astype(np.float32)
    large = max_exact + (np.log(df / max_exact) / math.log(2048 / max_exact) * (NB - max_exact)).astype(np.int32)
    large = np.minimum(large, NB - 1)
    return np.where(dist < max_exact, dist, large)


def _consts():
    oh = np.zeros((NB, 3, TL), np.float32)
    ohs = np.zeros((NB, 3, 128), np.float32)
    for p, dil in enumerate(DILS):
        for x in range(TL):
            delta = x - 127
            if 0 <= delta <= 128:
                oh[int(_bucket(np.array(delta * dil))), p, x] = 1.0
        for jj in range(128):
            dist = (128 - jj) * dil
            ohs[int(_bucket(np.array(dist))), p, jj] = 1.0
    return oh, ohs


class _Stop(Exception):
    pass


import os as _os
KSTOP = _os.environ.get("KSTOP", "")


_CK = [0]


_ST = [False]


def ck_auto():
    _CK[0] += 1
    if KSTOP and int(KSTOP) == _CK[0]:
        _ST[0] = True


class Sem:
    def __init__(self, nc, name):
        self.h = nc.alloc_semaphore(name)
        self.issued = 0


class Buf:
    __slots__ = ("writers", "readers")

    def __init__(self):
        self.writers = {}
        self.readers = {}


class EngQ:
    def __init__(self, nc, eng, name):
        self.eng = eng
        self.sem = Sem(nc, "e_" + name)
        self.count = 0
        self.pending = False
        self.waited = {}
        self.is_pe = (name == "pe")

    def wait(self, sem, val):
        if val <= 0:
            return
        if sem is self.sem:
            if self.is_pe or val > self.count:
                return
        if self.waited.get(sem, 0) >= val:
            return
        self.eng.wait_ge(sem.h, val)
        self.waited[sem] = val


def _deps(E, reads, writes):
    for b in reads:
        for s, v in b.writers.items():
            E.wait(s, v)
    for b in writes:
        for s, v in b.writers.items():
            E.wait(s, v)
        for s, v in b.readers.items():
            E.wait(s, v)


def op(E, fn, reads=(), writes=(), signal=True):
    if _ST[0]:
        return None
    _deps(E, reads, writes)
    inst = fn()
    cid = E.count + 1
    if signal:
        inst.then_inc(E.sem.h, 1)
        E.count = cid
        E.pending = False
    else:
        E.pending = True
    for b in writes:
        b.writers = {E.sem: cid}
        b.readers = {}
    for b in reads:
        if b.readers.get(E.sem, 0) < cid:
            b.readers[E.sem] = cid
    return inst


def dma(Q, sem, out, in_, reads=(), writes=()):
    if _ST[0]:
        return None
    _deps(Q, reads, writes)
    Q.wait(sem, sem.issued)
    inst = Q.eng.dma_start(out=out, in_=in_)
    inst.then_inc(sem.h, 16)
    sem.issued += 16
    for b in writes:
        b.writers = {sem: sem.issued}
        b.readers = {}
    for b in reads:
        b.readers[sem] = sem.issued
    return inst


class Ring:
    def __init__(self, items):
        self.items = items
        self.i = 0

    def next(self):
        it = self.items[self.i % len(self.items)]
        self.i += 1
        return it


def rap(base, off, dims):
    return bass.AP(tensor=base.tensor, offset=base.offset + off, ap=[list(base.ap[0])] + [list(d) for d in dims])


def build_nc():
    nc = bass.Bass("TRN2", target_bir_lowering=False)
    dt_in = lambda n, s, d=F32: nc.dram_tensor(n, list(s), d, kind="ExternalInput").ap()
    dt_out = lambda n, s, d=F32: nc.dram_tensor(n, list(s), d, kind="ExternalOutput").ap()
    _dbg = _os.environ.get("KDBG", "") == "1"
    dt_scr = lambda n, s, d=F32: nc.dram_tensor(n, list(s), d, kind=("ExternalOutput" if (_dbg and n in ("hbuf", "ybuf", "hmid", "ycT_d")) else "Internal")).ap()

    x_p = dt_in("x_p", [S, D])
    x_s = dt_in("x_s", [D])
    ck_in = dt_in("ck", [DEPTH, LW, ATT])
    cv_in = dt_in("cv", [DEPTH, LW, ATT])
    sc_in = dt_in("sc", [DEPTH, CK - 1, CC])
    pp_in = dt_in("pp", [DEPTH, S, PLE])
    ps_in = dt_in("psm", [DEPTH, PLE])
    relb = dt_in("rel_bias", [NB, H])
    g_pre = dt_in("g_pre", [DEPTH, D])
    w_in = dt_in("w_in", [DEPTH, D, INC])
    conv_w = dt_in("conv_w", [DEPTH, CK, CC])
    conv_b = dt_in("conv_b", [DEPTH, CC])
    ln_g = dt_in("ln_g", [DEPTH, CC])
    ln_b = dt_in("ln_b", [DEPTH, CC])
    w_out = dt_in("w_out", [DEPTH, D, D])
    g_post = dt_in("g_post", [DEPTH, D])
    w_ple = dt_in("w_ple", [DEPTH, PLE, D])
    g_ple = dt_in("g_ple", [DEPTH, D])
    w_pg = dt_in("w_pg", [DEPTH, D, D])
    b_pg = dt_in("b_pg", [DEPTH, D])
    oh_c = dt_in("oh_c", [NB, 3 * TL])
    ohs_c = dt_in("ohs_c", [NB, 3 * 128])
    ident_c = dt_in("ident_c", [128, 128])

    y_p = dt_out("y_p", [S, D])
    y_s = dt_out("y_s", [D])
    nk_p = dt_out("nk_p", [DEPTH, S, ATT])
    nv_p = dt_out("nv_p", [DEPTH, S, ATT])
    nc_p = dt_out("nc_p", [DEPTH, CK - 1, CC])
    nk_s = dt_out("nk_s", [DEPTH, LW, ATT])
    nv_s = dt_out("nv_s", [DEPTH, LW, ATT])
    nc_s = dt_out("nc_s", [DEPTH, CK - 1, CC])

    if _dbg:
        dbg_ssq = dt_out("dbg_ssq", [128, 256])
        dbg_rsy = dt_out("dbg_rsy", [128, 16])
        dbg_rsy2 = dt_out("dbg_rsy2", [128, 16])
        dbg_pT = dt_out("dbg_pT", [128, 2 * S], BF16)
        dbg_wpl = dt_out("dbg_wpl", [128, 2 * D], BF16)
    hbuf = dt_scr("hbuf", [S, D])
    ybuf = dt_scr("ybuf", [S, D])
    hmid = dt_scr("hmid", [S, D])
    qT_d = dt_scr("qT_d", [ATT, S], BF16)
    kT_d = dt_scr("kT_d", [ATT, S], BF16)
    gaT_d = dt_scr("gaT_d", [ATT, S], BF16)
    uT_d = dt_scr("uT_d", [CC, S], BF16)
    gbT_d = dt_scr("gbT_d", [CC, S], BF16)
    v_d = dt_scr("v_d", [S, ATT], BF16)
    ycT_d = dt_scr("ycT_d", [D, S], BF16)
    R_d = dt_scr("R_d", [H, 128, 3 * TL])

    PE = EngQ(nc, nc.tensor, "pe")
    ACT = EngQ(nc, nc.scalar, "act")
    DVE = EngQ(nc, nc.vector, "dve")
    PL = EngQ(nc, nc.gpsimd, "pool")
    SP = EngQ(nc, nc.sync, "sp")
    engs = [PE, ACT, DVE, PL, SP]
    all_sems = []

    def newsem(name):
        s = Sem(nc, name)
        all_sems.append(s)
        return s

    dregs = {}

    def dr(*key):
        b = dregs.get(key)
        if b is None:
            b = Buf()
            dregs[key] = b
        return b

    def barrier(include_pool=False):
        if _ST[0]:
            return
        es = [e for e in engs if include_pool or e is not PL]
        for e in es:
            for f in engs:
                if f is not e:
                    e.wait(f.sem, f.count)
            for s in all_sems:
                e.wait(s, s.issued)

    with ExitStack() as gs:
        _uid = [0]

        def sb(st, name, shape, dt):
            _uid[0] += 1
            return st.enter_context(nc.sbuf_tensor(f"{name}_{_uid[0]}", list(shape), dt))
        wbufs = []
        for i in range(2):
            wbufs.append((sb(gs, f"wbuf{i}", [128, KC, PW], BF16), Buf(), newsem(f"s_w{i}")))
        wring = Ring(wbufs)
        ident_b = sb(gs, "ident_b", [128, 128], BF16)
        ident_f = sb(gs, "ident_f", [128, 128], F32)
        ones_b = sb(gs, "ones_b", [128, 128], BF16)
        ones_f = sb(gs, "ones_f", [128, 128], F32)
        sel0 = sb(gs, "sel0", [NB, 128], F32)
        Eexp = sb(gs, "Eexp", [NB, H], F32)
        E0bc = sb(gs, "E0bc", [128, H], F32)
        EBs = sb(gs, "EBs", [128, 3, H], F32)
        hs = sb(gs, "hs", [128, KC], F32)
        xs_b = sb(gs, "xs_b", [128, KC], BF16)
        us_f = sb(gs, "us_f", [128, 112], F32)
        ycs_b = sb(gs, "ycs_b", [128, KC], BF16)
        ys_f = sb(gs, "ys_f", [128, KC], F32)
        es_f = sb(gs, "es_f", [128, KC], F32)
        pss_b = sb(gs, "pss_b", [128, 2], BF16)
        vec = sb(gs, "vec", [128, 8, KC], F32)
        cwT = sb(gs, "cwT", [128, 16, CK], F32)
        small = sb(gs, "small", [128, 64], F32)
        B_const = Buf()
        B_hs, B_xs, B_us, B_ycs, B_ys, B_es, B_pss, B_vec, B_cwT, B_small = (Buf() for _ in range(10))
        s_misc = newsem("s_misc")
        s_misc2 = newsem("s_misc2")
        s_st = [newsem(f"s_st{i}") for i in range(6)]
        s_ld = [newsem(f"s_ld{i}") for i in range(8)]

        psum = []
        for i in range(8):
            psum.append((gs.enter_context(nc.psum_tensor(f"ps{i}", [128, 512], F32)), Buf()))
        psA = Ring(psum[0:4])
        psB = Ring(psum[4:6])
        psC = Ring(psum[6:8])

        _CK[0] = 0
        _ST[0] = False
        try:
            op(DVE, lambda: nc.vector.memset(ones_b[:], 1.0), writes=[B_const])
            op(DVE, lambda: nc.vector.memset(ones_f[:], 1.0), writes=[B_const])
            op(DVE, lambda: nc.vector.memset(sel0[:], 0.0), writes=[B_const])
            op(DVE, lambda: nc.vector.memset(sel0[0:1, :], 1.0), writes=[B_const])
            dma(SP, s_misc, ident_f[:], ident_c[:, :], writes=[B_const])
            op(DVE, lambda: nc.vector.tensor_copy(ident_b[:], ident_f[:]), reads=[B_const], writes=[B_const])

            def load_piece(wsrc, col_ranges):
                t, b, s = wring.next()
                wv = wsrc.rearrange("(k p) c -> p k c", p=128)
                _deps(PL, (), [b])
                PL.wait(s, s.issued)
                for (c0, n, o) in col_ranges:
                    inst = nc.gpsimd.dma_start(out=t[:, :, o:o + n], in_=wv[:, :, c0:c0 + n])
                    inst.then_inc(s.h, 16)
                    s.issued += 16
                b.writers = {s: s.issued}
                b.readers = {}
                return t, b

            with ExitStack() as ph:
                rb = sb(ph, "rb", [NB, H], F32)
                ohc = sb(ph, "ohc", [NB, 3 * TL], F32)
                ohsc = sb(ph, "ohsc", [NB, 3 * 128], F32)
                Ebh = sb(ph, "Ebh", [NB, 128], F32)
                Rsb = sb(ph, "Rsb", [128, 3 * TL], F32)
                Bt, BE, BR = Buf(), Buf(), Buf()
                dma(SP, s_misc, rb[:], relb[:, :], writes=[Bt])
                dma(SP, s_misc, ohc[:], oh_c[:, :], writes=[Bt])
                dma(SP, s_misc, ohsc[:], ohs_c[:, :], writes=[Bt])
                op(ACT, lambda: nc.scalar.activation(out=Eexp[:], in_=rb[:], func=AF.Exp), reads=[Bt], writes=[B_const])
                pt, pb = psA.next()
                for p in range(3):
                    op(PE, lambda p=p: nc.tensor.matmul(pt[:, p * H:(p + 1) * H], lhsT=ohsc[:, p * 128:(p + 1) * 128], rhs=Eexp[:],
                                                        start=True, stop=True), reads=[Bt, B_const], writes=[pb])
                op(PE, lambda: nc.tensor.matmul(pt[:, 64:64 + H], lhsT=sel0[:], rhs=Eexp[:], start=True, stop=True),
                   reads=[B_const], writes=[pb])
                op(DVE, lambda: nc.vector.tensor_copy(EBs[:].rearrange("p a h -> p (a h)"), pt[:, 0:3 * H]), reads=[pb], writes=[B_const])
                op(DVE, lambda: nc.vector.tensor_copy(E0bc[:], pt[:, 64:64 + H]), reads=[pb], writes=[B_const])
                for h in range(H):
                    op(DVE, lambda h=h: nc.vector.tensor_copy(Ebh[:], rap(Eexp[:], h, [[0, 128]])), reads=[B_const], writes=[BE])
                    pts = [psA.next() for _ in range(3)]
                    for p in range(3):
                        op(PE, lambda p=p: nc.tensor.matmul(pts[p][0][:, 0:TL], lhsT=Ebh[:], rhs=ohc[:, p * TL:(p + 1) * TL],
                                                            start=True, stop=True), reads=[BE, Bt], writes=[pts[p][1]])
                    for p in range(3):
                        op(ACT, lambda p=p: nc.scalar.copy(out=Rsb[:, p * TL:(p + 1) * TL], in_=pts[p][0][:, 0:TL]),
                           reads=[pts[p][1]], writes=[BR])
                    dma(SP, s_misc2, R_d[h], Rsb[:], reads=[BR], writes=[dr("R", h)])
                barrier()
                ck_auto()

            with nc.allow_non_contiguous_dma(reason="tiny feature-major vector loads"):
                dma(SP, s_misc, hs[:], x_s.rearrange("(k p) -> p k", p=128), writes=[B_hs])
            for l in range(DEPTH):
                for (src, dst) in ((ck_in, nk_s), (cv_in, nv_s)):
                    dma(ACT, newsem(f"s_bk{l}_{id(dst) % 1000}a"), dst[l, 0:2032, :], src[l, 1:2033, :], writes=[dr("nks", id(dst), l, 0)])
                    dma(ACT, newsem(f"s_bk{l}_{id(dst) % 1000}b"), dst[l, 2032:2047, :], src[l, 2033:2048, :], writes=[dr("nks", id(dst), l, 1)])
                dma(ACT, newsem(f"s_bk{l}c"), nc_s[l, 0:CK - 2, :], sc_in[l, 1:CK - 1, :], writes=[dr("ncs", l)])

            for l in range(DEPTH):
                h_src = x_p if l == 0 else hbuf
                h_dst = hbuf if l == 0 else y_p
                hkey = "x" if l == 0 else "hbuf"
                hdkey = "hbuf" if l == 0 else "yp"

                with ExitStack() as ph:
                    vrow = sb(ph, "vrow", [128, 128], F32)
                    cwr = sb(ph, "cwr", [CK, CC], F32)
                    Bv, Bc = Buf(), Buf()
                    op(DVE, lambda: nc.vector.memset(vrow[:], 0.0), writes=[Bv])
                    dma(SP, s_misc, vrow[0:32, :], g_pre[l].rearrange("(k p) -> k p", p=128), writes=[Bv])
                    dma(SP, s_misc, vrow[32:48, :], conv_b[l].rearrange("(k p) -> k p", p=128), writes=[Bv])
                    dma(SP, s_misc, vrow[48:64, :], ln_g[l].rearrange("(k p) -> k p", p=128), writes=[Bv])
                    dma(SP, s_misc, vrow[64:80, :], ln_b[l].rearrange("(k p) -> k p", p=128), writes=[Bv])
                    dma(SP, s_misc2, cwr[:], conv_w[l], writes=[Bc])
                    pt, pb = psA.next()
                    op(PE, lambda: nc.tensor.transpose(pt[:, 0:128], vrow[:], ident_f[:]), reads=[Bv, B_const], writes=[pb])
                    op(DVE, lambda: nc.vector.tensor_copy(vec[:, 0, :], pt[:, 0:32]), reads=[pb], writes=[B_vec])
                    op(DVE, lambda: nc.vector.tensor_copy(vec[:, 1, 0:16], pt[:, 32:48]), reads=[pb], writes=[B_vec])
                    op(DVE, lambda: nc.vector.tensor_copy(vec[:, 2, 0:16], pt[:, 48:64]), reads=[pb], writes=[B_vec])
                    op(DVE, lambda: nc.vector.tensor_copy(vec[:, 3, 0:16], pt[:, 64:80]), reads=[pb], writes=[B_vec])
                    for c4 in range(4):
                        pt, pb = psA.next()
                        for j in range(4):
                            c = c4 * 4 + j
                            op(PE, lambda c=c, j=j: nc.tensor.transpose(pt[:, j * 32:j * 32 + CK], cwr[:, c * 128:(c + 1) * 128], ident_f[0:CK, 0:CK]),
                               reads=[Bc, B_const], writes=[pb])
                        op(DVE, lambda c4=c4: nc.vector.tensor_copy(cwT[:, c4 * 4:c4 * 4 + 4, :], pt[:, 0:128].rearrange("p (j t) -> p j t", t=32)[:, :, 0:CK]),
                           reads=[pb], writes=[B_cwT])
                    barrier()
                    ck_auto()
                gpreT = vec[:, 0, :]
                cbT = vec[:, 1, 0:16]
                lngT = vec[:, 2, 0:16]
                lnbT = vec[:, 3, 0:16]

                with ExitStack() as ph:
                    xnT = sb(ph, "xnT", [128, KC, S], BF16)
                    B_xn = [Buf() for _ in range(NT)]
                    with ExitStack() as ph0:
                        ht = sb(ph0, "ht", [128, D], F32)
                        xb = sb(ph0, "xb", [128, D], BF16)
                        st0 = sb(ph0, "st0", [128, 8], F32)
                        Bh, Bxb, Bst = Buf(), Buf(), Buf()
                        for t in range(NT):
                            dma(SP, s_ld[0], ht[:], h_src[t * 128:(t + 1) * 128, :], reads=[dr(hkey, t)], writes=[Bh])
                            op(ACT, lambda: nc.scalar.activation(out=xb[:], in_=ht[:], func=AF.Square, accum_out=st0[:, 0:1]),
                               reads=[Bh], writes=[Bxb, Bst])
                            op(DVE, lambda: nc.vector.tensor_scalar(out=st0[:, 1:2], in0=st0[:, 0:1], scalar1=1.0 / D, scalar2=EPS,
                                                                    op0=ALU.mult, op1=ALU.add), reads=[Bst], writes=[Bst])
                            op(ACT, lambda: nc.scalar.activation(out=st0[:, 2:3], in_=st0[:, 1:2], func=AF.Sqrt), reads=[Bst], writes=[Bst])
                            op(DVE, lambda: nc.vector.reciprocal(st0[:, 3:4], st0[:, 2:3]), reads=[Bst], writes=[Bst])
                            op(ACT, lambda: nc.scalar.activation(out=xb[:], in_=ht[:], func=AF.Copy, scale=st0[:, 3:4]),
                               reads=[Bh, Bst], writes=[Bxb])
                            for k4 in range(KC // 4):
                                pt, pb = psA.next()
                                ptb = pt[:].bitcast(BF16)
                                for j in range(4):
                                    k = k4 * 4 + j
                                    op(PE, lambda k=k, j=j: nc.tensor.transpose(ptb[:, j * 128:(j + 1) * 128], xb[:, k * 128:(k + 1) * 128], ident_b[:]),
                                       reads=[Bxb, B_const], writes=[pb], signal=(j == 3))
                                op(DVE, lambda k4=k4, t=t: nc.vector.tensor_tensor(
                                    out=xnT[:, k4 * 4:k4 * 4 + 4, t * 128:(t + 1) * 128],
                                    in0=ptb[:, 0:512].rearrange("p (j t) -> p j t", t=128),
                                    in1=rap(gpreT, k4 * 4, [[1, 4], [0, 128]]), op=ALU.mult),
                                   reads=[pb, B_vec], writes=[B_xn[t]])
                        op(DVE, lambda: nc.vector.tensor_tensor(out=small[:, 0:KC], in0=hs[:], in1=hs[:], op=ALU.mult), reads=[B_hs], writes=[B_small])
                        op(DVE, lambda: nc.vector.reduce_sum(out=small[:, 32:33], in_=small[:, 0:KC], axis=AX.X), reads=[B_small], writes=[B_small])
                        pt, pb = psA.next()
                        op(PE, lambda: nc.tensor.matmul(pt[:, 0:1], lhsT=ones_f[:], rhs=small[:, 32:33], start=True, stop=True),
                           reads=[B_small, B_const], writes=[pb])
                        op(DVE, lambda: nc.vector.tensor_scalar(out=small[:, 33:34], in0=pt[:, 0:1], scalar1=1.0 / D, scalar2=EPS,
                                                                op0=ALU.mult, op1=ALU.add), reads=[pb], writes=[B_small])
                        op(ACT, lambda: nc.scalar.activation(out=small[:, 34:35], in_=small[:, 33:34], func=AF.Sqrt), reads=[B_small], writes=[B_small])
                        op(DVE, lambda: nc.vector.reciprocal(small[:, 35:36], small[:, 34:35]), reads=[B_small], writes=[B_small])
                        op(DVE, lambda: nc.vector.scalar_tensor_tensor(out=xs_b[:], in0=hs[:], scalar=small[:, 35:36], in1=gpreT,
                                                                       op0=ALU.mult, op1=ALU.mult), reads=[B_hs, B_small, B_vec], writes=[B_xs])
                        barrier()
                        ck_auto()

                    with ExitStack() as ph1:
                        stb = [(sb(ph1, f"stb{i}", [128, 512], BF16), Buf(), s_st[i]) for i in range(2)]
                        stf = [(sb(ph1, f"stf{i}", [128, 512], F32), Buf(), s_st[2 + i]) for i in range(2)]
                        stbr, stfr = Ring(stb), Ring(stf)
                        glu = sb(ph1, "glu", [128, S], F32)
                        Bglu = [Buf() for _ in range(4)]
                        ufp = sb(ph1, "ufp", [128, 512], F32)
                        Bufp = Buf()
                        sig = sb(ph1, "sig", [128, 512], F32)
                        Bsig = Buf()
                        cst = sb(ph1, "cst", [32, CC], F32)
                        Bcst = Buf()

                        def fm_chunk(wt, wb, off, kind, ci):
                            for tb in range(4):
                                pt, pb = psA.next()
                                for k in range(KC):
                                    op(PE, lambda k=k: nc.tensor.matmul(pt[:], lhsT=wt[:, k, off:off + 128], rhs=xnT[:, k, tb * 512:(tb + 1) * 512],
                                                                        start=(k == 0), stop=(k == KC - 1)),
                                       reads=[wb] + B_xn[tb * 4:tb * 4 + 4], writes=[pb], signal=(k == KC - 1))
                                rows = slice(ci * 128, (ci + 1) * 128)
                                cols = slice(tb * 512, (tb + 1) * 512)
                                if kind == "q":
                                    st, sbf, ss = stbr.next()
                                    op(ACT, lambda: nc.scalar.activation(out=st[:], in_=pt[:], func=AF.Copy, scale=HD ** -0.5), reads=[pb], writes=[sbf])
                                    dma(SP, ss, qT_d[rows, cols], st[:], reads=[sbf], writes=[dr("qT", ci, tb)])
                                elif kind == "k":
                                    st, sbf, ss = stbr.next()
                                    op(DVE, lambda: nc.vector.tensor_copy(st[:], pt[:]), reads=[pb], writes=[sbf])
                                    dma(SP, ss, kT_d[rows, cols], st[:], reads=[sbf], writes=[dr("kT", ci, tb)])
                                elif kind in ("ga", "gb"):
                                    st, sbf, ss = stbr.next()
                                    op(ACT, lambda: nc.scalar.activation(out=st[:], in_=pt[:], func=AF.Silu), reads=[pb], writes=[sbf])
                                    dst = gaT_d if kind == "ga" else gbT_d
                                    dma(SP, ss, dst[rows, cols], st[:], reads=[sbf], writes=[dr(kind, ci, tb)])
                                elif kind == "glua":
                                    op(DVE, lambda: nc.vector.tensor_copy(glu[:, cols], pt[:]), reads=[pb], writes=[Bglu[tb]])
                                elif kind == "glub":
                                    op(ACT, lambda: nc.scalar.activation(out=sig[:], in_=pt[:], func=AF.Sigmoid), reads=[pb], writes=[Bsig])
                                    op(DVE, lambda: nc.vector.tensor_tensor(out=ufp[:], in0=glu[:, cols], in1=sig[:], op=ALU.mult),
                                       reads=[Bglu[tb], Bsig], writes=[Bufp])
                                    st, sbf, ss = stbr.next()
                                    op(ACT, lambda: nc.scalar.copy(out=st[:], in_=ufp[:]), reads=[Bufp], writes=[sbf])
                                    dma(SP, ss, uT_d[rows, cols], st[:], reads=[sbf], writes=[dr("uT", ci, tb)])
                                    if tb == 3:
                                        p2, p2b = psC.next()
                                        op(PE, lambda: nc.tensor.transpose(p2[0:CK - 1, 0:128], ufp[:, 512 - (CK - 1):512], ident_f[:]),
                                           reads=[Bufp, B_const], writes=[p2b])
                                        op(DVE, lambda: nc.vector.tensor_copy(cst[0:CK - 1, rows], p2[0:CK - 1, 0:128]), reads=[p2b], writes=[Bcst])

                        def fm_sample(wt, wb, off, col):
                            pt, pb = psB.next()
                            for k in range(KC):
                                op(PE, lambda k=k: nc.tensor.matmul(pt[:, 0:1], lhsT=wt[:, k, off:off + 128], rhs=xs_b[:, k:k + 1],
                                                                    start=(k == 0), stop=(k == KC - 1)),
                                   reads=[wb, B_xs], writes=[pb], signal=(k == KC - 1))
                            op(DVE, lambda: nc.vector.tensor_copy(us_f[:, col:col + 1], pt[:, 0:1]), reads=[pb], writes=[B_us])

                        def tm_piece(wt, wb, c0, dst_out, want_v):
                            for t in range(NT):
                                pt, pb = psA.next()
                                for k in range(KC):
                                    op(PE, lambda k=k: nc.tensor.matmul(pt[:, 0:PW], lhsT=xnT[:, k, t * 128:(t + 1) * 128], rhs=wt[:, k, :],
                                                                        start=(k == 0), stop=(k == KC - 1)),
                                       reads=[wb, B_xn[t]], writes=[pb], signal=(k == KC - 1))
                                _ktm = int(_os.environ.get("KTM", "9"))
                                if _ktm < 2:
                                    continue
                                st, sbf, ss = stfr.next()
                                op(ACT, lambda: nc.scalar.copy(out=st[:, 0:PW], in_=pt[:, 0:PW]), reads=[pb], writes=[sbf])
                                if _ktm < 3:
                                    continue
                                _dd = {"1": ybuf[t * 128:(t + 1) * 128, c0:c0 + PW], "2": y_p[t * 128:(t + 1) * 128, c0:c0 + PW], "3": nv_p[l, t * 128:(t + 1) * 128, c0:c0 + PW]}.get(_os.environ.get("KV1", ""), dst_out[l, t * 128:(t + 1) * 128, c0:c0 + PW])
                                dma(SP, ss, _dd, st[:, 0:PW], reads=[sbf], writes=[dr("nkv", id(dst_out), l, t, c0)])
                                if want_v:
                                    st2, sbf2, ss2 = stbr.next()
                                    op(DVE, lambda: nc.vector.tensor_copy(st2[:, 0:PW], st[:, 0:PW]), reads=[sbf], writes=[sbf2])
                                    dma(SP, ss2, v_d[t * 128:(t + 1) * 128, c0:c0 + PW], st2[:, 0:PW], reads=[sbf2], writes=[dr("v", t, c0 // 128), dr("v", t, c0 // 128 + 1)])

                        wsrc = w_in[l]
                        sched = []
                        for i in range(8):
                            sched.append(("k", 2048 + i * PW))
                        for i in range(8):
                            sched.append(("v", 4096 + i * PW))
                        for i in range(8):
                            sched.append(("q", i * PW))
                        for i in range(8):
                            sched.append(("ga", 6144 + i * PW))
                        for i in range(8):
                            sched.append(("glu", i))
                        for i in range(8):
                            sched.append(("gb", 12288 + i * PW))

                        sched2 = []
                        for kind, a in sched:
                            if kind == "glu":
                                sched2.append(("glu", 2 * a))
                                sched2.append(("glu", 2 * a + 1))
                            else:
                                sched2.append((kind, a))

                        def issue2(item):
                            kind, a = item
                            if kind == "glu":
                                return load_piece(wsrc, [(8192 + a * 128, 128, 0), (10240 + a * 128, 128, 128)])
                            return load_piece(wsrc, [(a, PW, 0)])

                        _np = int(_os.environ.get("KPIECES", "0"))
                        if _np:
                            sched2 = sched2[:_np]
                        _sk = int(_os.environ.get("KSKIP", "0"))
                        if _sk:
                            sched2 = sched2[_sk:]
                        if _os.environ.get("KNOSAMPLE", "") == "1":
                            fm_sample = lambda *a, **k: None
                        if _os.environ.get("KNOTM", "") == "1":
                            tm_piece = lambda *a, **k: None
                        if _os.environ.get("KNOFM", "") == "1":
                            fm_chunk = lambda *a, **k: None
                        pend = [issue2(sched2[0])]
                        for i, item in enumerate(sched2):
                            if i + 1 < len(sched2):
                                pend.append(issue2(sched2[i + 1]))
                            wt, wb = pend.pop(0)
                            kind, a = item
                            if kind == "k":
                                tm_piece(wt, wb, a - 2048, nk_p, False)
                                for j in range(2):
                                    ci = (a - 2048) // 128 + j
                                    fm_chunk(wt, wb, j * 128, "k", ci)
                                    fm_sample(wt, wb, j * 128, 16 + ci)
                            elif kind == "v":
                                tm_piece(wt, wb, a - 4096, nv_p, True)
                                for j in range(2):
                                    ci = (a - 4096) // 128 + j
                                    fm_sample(wt, wb, j * 128, 32 + ci)
                            elif kind == "q":
                                for j in range(2):
                                    ci = a // 128 + j
                                    fm_chunk(wt, wb, j * 128, "q", ci)
                                    fm_sample(wt, wb, j * 128, ci)
                            elif kind == "ga":
                                for j in range(2):
                                    ci = (a - 6144) // 128 + j
                                    fm_chunk(wt, wb, j * 128, "ga", ci)
                                    fm_sample(wt, wb, j * 128, 48 + ci)
                            elif kind == "glu":
                                fm_chunk(wt, wb, 0, "glua", a)
                                fm_sample(wt, wb, 0, 64 + a)
                                fm_chunk(wt, wb, 128, "glub", a)
                                fm_sample(wt, wb, 128, 80 + a)
                            elif kind == "gb":
                                for j in range(2):
                                    ci = (a - 12288) // 128 + j
                                    fm_chunk(wt, wb, j * 128, "gb", ci)
                                    fm_sample(wt, wb, j * 128, 96 + ci)

                        op(ACT, lambda: nc.scalar.activation(out=small[:, 0:16], in_=us_f[:, 80:96], func=AF.Sigmoid), reads=[B_us], writes=[B_small])
                        op(DVE, lambda: nc.vector.tensor_tensor(out=us_f[:, 64:80], in0=us_f[:, 64:80], in1=small[:, 0:16], op=ALU.mult),
                           reads=[B_us, B_small], writes=[B_us])
                        op(DVE, lambda: nc.vector.tensor_scalar(out=us_f[:, 0:16], in0=us_f[:, 0:16], scalar1=HD ** -0.5, scalar2=None, op0=ALU.mult),
                           reads=[B_us], writes=[B_us])
                        dma(SP, s_misc, nc_p[l], cst[0:CK - 1, :], reads=[Bcst], writes=[dr("ncp", l)])
                        usr = sb(ph1, "usr", [112, 128], F32)
                        Busr = Buf()
                        ptr_, pbr_ = psA.next()
                        op(PE, lambda: nc.tensor.transpose(ptr_[0:112, 0:128], us_f[:, 0:112], ident_f[:]), reads=[B_us, B_const], writes=[pbr_])
                        op(DVE, lambda: nc.vector.tensor_copy(usr[:], ptr_[0:112, 0:128]), reads=[pbr_], writes=[Busr])
                        dma(SP, s_misc, nc_s[l, CK - 2, :].rearrange("(c p) -> c p", p=128), usr[64:80, :], reads=[Busr], writes=[dr("ncs_row", l)])
                        dma(SP, s_misc, nk_s[l, LW - 1, :].rearrange("(c p) -> c p", p=128), usr[16:32, :], reads=[Busr], writes=[dr("nks_row", l, 0)])
                        dma(SP, s_misc, nv_s[l, LW - 1, :].rearrange("(c p) -> c p", p=128), usr[32:48, :], reads=[Busr], writes=[dr("nks_row", l, 1)])
                        barrier()
                        ck_auto()

                with ExitStack() as ph:
                    NBUF = 2
                    hin = []
                    for i in range(NBUF):
                        hin.append(dict(
                            q=sb(ph, f"aq{i}", [128, S], BF16), k=sb(ph, f"ak{i}", [128, S], BF16), ga=sb(ph, f"ag{i}", [128, S], BF16),
                            v1=sb(ph, f"av1{i}", [128, 16, 128], BF16), v4=sb(ph, f"av4{i}", [128, 16, 128], BF16),
                            v16=sb(ph, f"av16{i}", [128, 16, 128], BF16), tb=sb(ph, f"atb{i}", [128, 3, 256], F32),
                            b=Buf(), s=s_ld[i]))
                    ex = [(sb(ph, f"ex{i}", [128, 512], F32), Buf()) for i in range(3)]
                    exr = Ring(ex)
                    pp = [(sb(ph, f"pp{i}", [128, 512], BF16), Buf()) for i in range(4)]
                    ppr = Ring(pp)
                    rd = sb(ph, "rd", [128, 512], F32)
                    of = sb(ph, "of", [128, 512], F32)
                    Brd, Bof = Buf(), Buf()
                    yst = [(sb(ph, f"yst{i}", [128, 512], BF16), Buf(), s_st[i]) for i in range(2)]
                    ystr = Ring(yst)
                    kc = sb(ph, "kc", [128, ATT], F32)
                    vc = sb(ph, "vc", [128, ATT], F32)
                    kcT = sb(ph, "kcT", [128, H, 128], F32)
                    Bkc, Bvc, BkcT = Buf(), Buf(), Buf()
                    sm = sb(ph, "sm", [128, 128], F32)
                    Bsm = Buf()

                    def load_head(h, slot):
                        d = hin[slot]
                        rows = slice(h * 128, (h + 1) * 128)
                        b, s = d["b"], d["s"]
                        rd_ = [dr("qT", h, tb) for tb in range(4)] + [dr("kT", h, tb) for tb in range(4)] + [dr("ga", h, tb) for tb in range(4)] \
                            + [dr("v", t, h) for t in range(NT)] + [dr("R", h)]
                        _deps(SP, rd_, [b])
                        SP.wait(s, s.issued)
                        insts = []
                        insts.append(nc.sync.dma_start(out=d["q"][:], in_=qT_d[rows, :]))
                        insts.append(nc.sync.dma_start(out=d["k"][:], in_=kT_d[rows, :]))
                        insts.append(nc.sync.dma_start(out=d["ga"][:], in_=gaT_d[rows, :]))
                        vh = v_d[:, rows]
                        insts.append(nc.sync.dma_start(out=d["v1"][:], in_=vh.rearrange("(n j) e -> j n e", j=128)))
                        for m_ in range(4):
                            insts.append(nc.sync.dma_start(out=d["v4"][:, m_ * 4:(m_ + 1) * 4, :],
                                                           in_=vh[m_ * 512:(m_ + 1) * 512, :].rearrange("(j r) e -> j r e", r=4)))
                        insts.append(nc.sync.dma_start(out=d["v16"][:], in_=vh.rearrange("(j r) e -> j r e", r=16)))
                        for p in range(3):
                            src = bass.AP(tensor=R_d.tensor, offset=R_d[h].offset + p * TL + 127, ap=[[3 * TL - 1, 128], [1, 256]])
                            insts.append(nc.sync.dma_start(out=d["tb"][:, p, :], in_=src))
                        for ins in insts:
                            ins.then_inc(s.h, 16)
                            s.issued += 16
                        b.writers = {s: s.issued}
                        b.readers = {}
                        for x in rd_:
                            x.readers[s] = s.issued

                    load_head(0, 0)
                    for h in range(H):
                        if h + 1 < H:
                            load_head(h + 1, (h + 1) % NBUF)
                        d = hin[h % NBUF]
                        hb = d["b"]
                        q, k, ga, tbm = d["q"], d["k"], d["ga"], d["tb"]
                        for c in range(4):
                            po, pob = psB.next()
                            pd, pdb = psC.next()
                            first = [True]

                            def acc(out_o, out_d, vt, prhs, pbuf):
                                st_ = first[0]
                                first[0] = False
                                op(PE, lambda: nc.tensor.matmul(out_o, lhsT=vt, rhs=prhs, start=st_, stop=False, skip_group_check=True),
                                   reads=[hb, pbuf], writes=[pob], signal=False)
                                op(PE, lambda: nc.tensor.matmul(out_d, lhsT=ones_b[:], rhs=prhs, start=st_, stop=False, skip_group_check=True),
                                   reads=[B_const, pbuf], writes=[pdb], signal=True)

                            for half in range(2):
                                ps_, psb_ = psA.next()
                                blocks = []
                                for jj in range(2):
                                    n = c * 4 + half * 2 + jj
                                    qs = q[:, n * 128:(n + 1) * 128]
                                    op(PE, lambda: nc.tensor.matmul(ps_[:, jj * 256:jj * 256 + 128], lhsT=k[:, n * 128:(n + 1) * 128], rhs=qs,
                                                                    start=True, stop=True, skip_group_check=True), reads=[hb], writes=[psb_], signal=(n == 0 and jj == 1))
                                    if n > 0:
                                        op(PE, lambda: nc.tensor.matmul(ps_[:, jj * 256 + 128:jj * 256 + 256], lhsT=k[:, (n - 1) * 128:n * 128], rhs=qs,
                                                                        start=True, stop=True, skip_group_check=True), reads=[hb], writes=[psb_], signal=(jj == 1))
                                    blocks.append(n)
                                e_, eb_ = exr.next()
                                p_, pb_ = ppr.next()
                                if blocks[0] == 0:
                                    op(ACT, lambda: nc.scalar.activation(out=e_[:, 0:128], in_=ps_[:, 0:128], func=AF.Exp), reads=[psb_], writes=[eb_])
                                    op(ACT, lambda: nc.scalar.activation(out=e_[:, 256:512], in_=ps_[:, 256:512], func=AF.Exp), reads=[psb_], writes=[eb_])
                                    op(DVE, lambda: nc.vector.tensor_tensor(out=p_[:, 0:128], in0=e_[:, 0:128], in1=tbm[:, 0, 0:128], op=ALU.mult),
                                       reads=[eb_, hb], writes=[pb_])
                                    op(DVE, lambda: nc.vector.tensor_tensor(out=p_[:, 256:512], in0=e_[:, 256:512], in1=tbm[:, 0, :], op=ALU.mult),
                                       reads=[eb_, hb], writes=[pb_])
                                else:
                                    op(ACT, lambda: nc.scalar.activation(out=e_[:], in_=ps_[:], func=AF.Exp), reads=[psb_], writes=[eb_])
                                    op(DVE, lambda: nc.vector.tensor_tensor(out=p_[:].rearrange("p (a x) -> p a x", a=2),
                                                                            in0=e_[:].rearrange("p (a x) -> p a x", a=2),
                                                                            in1=rap(tbm[:, 0, :], 0, [[0, 2], [1, 256]]), op=ALU.mult),
                                       reads=[eb_, hb], writes=[pb_])
                                for jj, n in enumerate(blocks):
                                    oc = (n - c * 4) * 128
                                    acc(po[:, oc:oc + 128], pd[:, oc:oc + 128], d["v1"][:, n, :], p_[:, jj * 256:jj * 256 + 128], pb_)
                                    if n > 0:
                                        acc(po[:, oc:oc + 128], pd[:, oc:oc + 128], d["v1"][:, n - 1, :], p_[:, jj * 256 + 128:jj * 256 + 256], pb_)
                            for half in range(2):
                                ps_, psb_ = psA.next()
                                for jj in range(2):
                                    r = half * 2 + jj
                                    qs = rap(q[:], c * 512 + r, [[4, 128]])
                                    op(PE, lambda: nc.tensor.matmul(ps_[:, jj * 256:jj * 256 + 128], lhsT=rap(k[:], c * 512 + r, [[4, 128]]), rhs=qs,
                                                                    start=True, stop=True, skip_group_check=True), reads=[hb], writes=[psb_], signal=(c == 0 and jj == 1))
                                    if c > 0:
                                        op(PE, lambda: nc.tensor.matmul(ps_[:, jj * 256 + 128:jj * 256 + 256], lhsT=rap(k[:], (c - 1) * 512 + r, [[4, 128]]), rhs=qs,
                                                                        start=True, stop=True, skip_group_check=True), reads=[hb], writes=[psb_], signal=(jj == 1))
                                e_, eb_ = exr.next()
                                p_, pb_ = ppr.next()
                                if c == 0:
                                    for jj in range(2):
                                        op(ACT, lambda jj=jj: nc.scalar.activation(out=e_[:, jj * 256:jj * 256 + 128], in_=ps_[:, jj * 256:jj * 256 + 128], func=AF.Exp),
                                           reads=[psb_], writes=[eb_])
                                        op(DVE, lambda jj=jj: nc.vector.tensor_tensor(out=p_[:, jj * 256:jj * 256 + 128], in0=e_[:, jj * 256:jj * 256 + 128],
                                                                                      in1=tbm[:, 1, 0:128], op=ALU.mult), reads=[eb_, hb], writes=[pb_])
                                else:
                                    op(ACT, lambda: nc.scalar.activation(out=e_[:], in_=ps_[:], func=AF.Exp), reads=[psb_], writes=[eb_])
                                    op(DVE, lambda: nc.vector.tensor_tensor(out=p_[:].rearrange("p (a x) -> p a x", a=2),
                                                                            in0=e_[:].rearrange("p (a x) -> p a x", a=2),
                                                                            in1=rap(tbm[:, 1, :], 0, [[0, 2], [1, 256]]), op=ALU.mult),
                                       reads=[eb_, hb], writes=[pb_])
                                for jj in range(2):
                                    r = half * 2 + jj
                                    oo = rap(po[:], r, [[4, 128]])
                                    od = rap(pd[:], r, [[4, 128]])
                                    acc(oo, od, d["v4"][:, c * 4 + r, :], p_[:, jj * 256:jj * 256 + 128], pb_)
                                    if c > 0:
                                        acc(oo, od, d["v4"][:, (c - 1) * 4 + r, :], p_[:, jj * 256 + 128:jj * 256 + 256], pb_)
                            ps_, psb_ = psA.next()
                            for r in range(16):
                                op(PE, lambda r=r: nc.tensor.matmul(ps_[:, r * 32:(r + 1) * 32], lhsT=rap(k[:], r, [[16, 128]]),
                                                                    rhs=rap(q[:], c * 512 + r, [[16, 32]]), start=True, stop=True, skip_group_check=True),
                                   reads=[hb], writes=[psb_], signal=(r == 15))
                            e_, eb_ = exr.next()
                            p_, pb_ = ppr.next()
                            op(ACT, lambda: nc.scalar.activation(out=e_[:], in_=ps_[:], func=AF.Exp), reads=[psb_], writes=[eb_])
                            op(DVE, lambda: nc.vector.tensor_tensor(out=p_[:].rearrange("p (a x) -> p a x", a=16),
                                                                    in0=e_[:].rearrange("p (a x) -> p a x", a=16),
                                                                    in1=rap(tbm[:, 2, :], c * 32, [[0, 16], [1, 32]]), op=ALU.mult),
                               reads=[eb_, hb], writes=[pb_])
                            for r in range(16):
                                acc(rap(po[:], r, [[16, 32]]), rap(pd[:], r, [[16, 32]]), d["v16"][:, r, :], p_[:, r * 32:(r + 1) * 32], pb_)
                            op(DVE, lambda: nc.vector.reciprocal(rd[:], pd[:]), reads=[pdb], writes=[Brd])
                            op(DVE, lambda: nc.vector.tensor_tensor(out=of[:], in0=po[:], in1=rd[:], op=ALU.mult), reads=[pob, Brd], writes=[Bof])
                            st, sbf, ss = ystr.next()
                            op(DVE, lambda: nc.vector.tensor_tensor(out=st[:], in0=of[:], in1=ga[:, c * 512:(c + 1) * 512], op=ALU.mult),
                               reads=[Bof, hb], writes=[sbf])
                            dma(SP, ss, ycT_d[h * 128:(h + 1) * 128, c * 512:(c + 1) * 512], st[:], reads=[sbf], writes=[dr("ycT", h, c)])

                    qs_ = us_f[:, 0:16]
                    ks_ = us_f[:, 16:32]
                    vs_ = us_f[:, 32:48]
                    pso, psob = psB.next()
                    psd, psdb = psC.next()
                    firsto = True
                    for p, dil in enumerate(DILS):
                        r0 = LW - 128 * dil
                        ksrc = bass.AP(tensor=ck_in.tensor, offset=ck_in[l].offset + r0 * ATT, ap=[[dil * ATT, 128], [1, ATT]])
                        vsrc = bass.AP(tensor=cv_in.tensor, offset=cv_in[l].offset + r0 * ATT, ap=[[dil * ATT, 128], [1, ATT]])
                        dma(SP, s_ld[2], kc[:], ksrc, writes=[Bkc])
                        dma(SP, s_ld[3], vc[:], vsrc, writes=[Bvc])
                        for h4 in range(4):
                            pt, pb = psA.next()
                            for j in range(4):
                                hh = h4 * 4 + j
                                op(PE, lambda hh=hh, j=j: nc.tensor.transpose(pt[:, j * 128:(j + 1) * 128], kc[:, hh * 128:(hh + 1) * 128], ident_f[:]),
                                   reads=[Bkc, B_const], writes=[pb])
                            op(ACT, lambda h4=h4: nc.scalar.copy(out=kcT[:, h4 * 4:h4 * 4 + 4, :].rearrange("p a x -> p (a x)"), in_=pt[:]),
                               reads=[pb], writes=[BkcT])
                        pt, pb = psA.next()
                        for hh in range(H):
                            op(PE, lambda hh=hh: nc.tensor.matmul(pt[:, hh:hh + 1], lhsT=kcT[:, hh, :], rhs=qs_[:, hh:hh + 1], start=True, stop=True,
                                                                  skip_group_check=True), reads=[BkcT, B_us], writes=[pb])
                        op(ACT, lambda: nc.scalar.activation(out=sm[:, 0:16], in_=pt[:, 0:16], func=AF.Exp), reads=[pb], writes=[Bsm])
                        op(DVE, lambda p=p: nc.vector.tensor_tensor(out=sm[:, 16:32], in0=sm[:, 0:16], in1=EBs[:, p, :], op=ALU.mult),
                           reads=[Bsm, B_const], writes=[Bsm])
                        for hh in range(H):
                            st_ = firsto
                            firsto = False
                            op(PE, lambda hh=hh, st_=st_: nc.tensor.matmul(pso[:, hh:hh + 1], lhsT=vc[:, hh * 128:(hh + 1) * 128], rhs=sm[:, 16 + hh:17 + hh],
                                                                           start=st_, stop=False, skip_group_check=True), reads=[Bvc, Bsm], writes=[psob])
                        op(PE, lambda p=p: nc.tensor.matmul(psd[:, 0:16], lhsT=ones_f[:], rhs=sm[:, 16:32], start=(p == 0), stop=False, skip_group_check=True),
                           reads=[Bsm, B_const], writes=[psdb])
                    op(DVE, lambda: nc.vector.tensor_tensor(out=sm[:, 32:48], in0=qs_, in1=ks_, op=ALU.mult), reads=[B_us], writes=[Bsm])
                    pt, pb = psA.next()
                    op(PE, lambda: nc.tensor.matmul(pt[:, 0:16], lhsT=ones_f[:], rhs=sm[:, 32:48], start=True, stop=True), reads=[Bsm, B_const], writes=[pb])
                    op(ACT, lambda: nc.scalar.activation(out=sm[:, 48:64], in_=pt[:, 0:16], func=AF.Exp), reads=[pb], writes=[Bsm])
                    op(DVE, lambda: nc.vector.scalar_tensor_tensor(out=sm[:, 48:64], in0=sm[:, 48:64], scalar=3.0, in1=E0bc[:], op0=ALU.mult, op1=ALU.mult),
                       reads=[Bsm, B_const], writes=[Bsm])
                    op(DVE, lambda: nc.vector.tensor_tensor(out=sm[:, 64:80], in0=sm[:, 48:64], in1=vs_, op=ALU.mult), reads=[Bsm, B_us], writes=[Bsm])
                    op(DVE, lambda: nc.vector.tensor_tensor(out=sm[:, 64:80], in0=sm[:, 64:80], in1=pso[:, 0:16], op=ALU.add), reads=[Bsm, psob], writes=[Bsm])
                    op(DVE, lambda: nc.vector.tensor_tensor(out=sm[:, 80:96], in0=sm[:, 48:64], in1=psd[:, 0:16], op=ALU.add), reads=[Bsm, psdb], writes=[Bsm])
                    op(DVE, lambda: nc.vector.reciprocal(sm[:, 96:112], sm[:, 80:96]), reads=[Bsm], writes=[Bsm])
                    op(DVE, lambda: nc.vector.tensor_tensor(out=sm[:, 64:80], in0=sm[:, 64:80], in1=sm[:, 96:112], op=ALU.mult), reads=[Bsm], writes=[Bsm])
                    op(ACT, lambda: nc.scalar.activation(out=sm[:, 112:128], in_=us_f[:, 48:64], func=AF.Silu), reads=[B_us], writes=[Bsm])
                    op(DVE, lambda: nc.vector.tensor_tensor(out=ycs_b[:, 0:16], in0=sm[:, 64:80], in1=sm[:, 112:128], op=ALU.mult), reads=[Bsm], writes=[B_ycs])
                    barrier()
                    ck_auto()

                with ExitStack() as ph:
                    HALO = CK - 1
                    ut = [(sb(ph, f"ut{i}", [128, 16, HALO + 512], BF16), Buf(), s_ld[i]) for i in range(2)]
                    gbt = [(sb(ph, f"gbt{i}", [128, 16, 512], BF16), Buf(), s_ld[2 + i]) for i in range(2)]
                    dg = [(sb(ph, f"dg{i}", [128, CK, 128], BF16), Buf()) for i in range(2)]
                    dgr = Ring(dg)
                    identb3 = rap(ident_b[:], 0, [[0, CK], [1, 128]])
                    cwTb = sb(ph, "cwTb", [128, 16, CK], BF16)
                    BcwTb = Buf()
                    op(DVE, lambda: nc.vector.tensor_copy(cwTb[:], cwT[:]), reads=[B_cwT], writes=[BcwTb])
                    yc = sb(ph, "yc", [128, 16, 512], F32)
                    Byc = [Buf() for _ in range(16)]
                    ybq = [(sb(ph, f"ybq{i}", [128, 512], BF16), Buf()) for i in range(2)]
                    ysq = [(sb(ph, f"ysq{i}", [128, 512], BF16), Buf()) for i in range(2)]
                    ybr, ysr = Ring(ybq), Ring(ysq)
                    mean = sb(ph, "mean", [128, 512], F32)
                    rstd = sb(ph, "rstd", [128, 512], F32)
                    tmpc = sb(ph, "tmpc", [128, 512], F32)
                    Bmean, Brstd, Btmp = Buf(), Buf(), Buf()
                    tn = [(sb(ph, f"tn{i}", [128, 512], F32), Buf()) for i in range(2)]
                    tnr = Ring(tn)
                    sl = [(sb(ph, f"sl{i}", [128, 512], F32), Buf()) for i in range(2)]
                    slr = Ring(sl)
                    yst = [(sb(ph, f"cyst{i}", [128, 512], BF16), Buf(), s_st[i]) for i in range(2)]
                    ystr = Ring(yst)

                    def load_tb(tb):
                        u_, ub_, us_ = ut[tb % 2]
                        g_, gb_, gs_ = gbt[tb % 2]
                        uv = uT_d.rearrange("(c p) t -> p c t", p=128)
                        gv = gbT_d.rearrange("(c p) t -> p c t", p=128)
                        rdu = [dr("uT", ci, x) for ci in range(16) for x in range(4)]
                        if tb == 0:
                            op(DVE, lambda: nc.vector.memset(u_[:, :, 0:HALO], 0.0), writes=[ub_])
                            dma(SP, us_, u_[:, :, HALO:HALO + 512], uv[:, :, 0:512], reads=rdu, writes=[ub_])
                        else:
                            dma(SP, us_, u_[:], uv[:, :, tb * 512 - HALO:(tb + 1) * 512], reads=rdu, writes=[ub_])
                        dma(SP, gs_, g_[:], gv[:, :, tb * 512:(tb + 1) * 512], reads=[dr("gb", ci, x) for ci in range(16) for x in range(4)], writes=[gb_])

                    load_tb(0)
                    for tb in range(4):
                        if tb + 1 < 4:
                            load_tb(tb + 1)
                        u_, ub_, _ = ut[tb % 2]
                        g_, gb_, _ = gbt[tb % 2]
                        psm, psmb = psB.next()
                        pss, pssb = psC.next()
                        for c in range(16):
                            dgt, dgb = dgr.next()
                            op(PL, lambda c=c: nc.gpsimd.tensor_tensor(out=dgt[:], in0=identb3, in1=rap(cwTb[:, c, :], 0, [[1, CK], [0, 128]]), op=ALU.mult),
                               reads=[BcwTb, B_const], writes=[dgb])
                            pt, pb = psA.next()
                            for j in range(CK):
                                op(PE, lambda j=j: nc.tensor.matmul(pt[:], lhsT=dgt[:, j, :], rhs=u_[:, c, j:j + 512], start=(j == 0), stop=(j == CK - 1)),
                                   reads=[dgb, ub_], writes=[pb], signal=(j == CK - 1))
                            op(ACT, lambda c=c: nc.scalar.activation(out=yc[:, c, :], in_=pt[:], func=AF.Identity, bias=cbT[:, c:c + 1]),
                               reads=[pb, B_vec], writes=[Byc[c]])
                            yb_, ybb_ = ybr.next()
                            ys_, ysb_ = ysr.next()
                            op(DVE, lambda c=c: nc.vector.tensor_copy(yb_[:], yc[:, c, :]), reads=[Byc[c]], writes=[ybb_])
                            op(ACT, lambda c=c: nc.scalar.activation(out=ys_[:], in_=yc[:, c, :], func=AF.Square), reads=[Byc[c]], writes=[ysb_])
                            op(PE, lambda c=c: nc.tensor.matmul(psm[:], lhsT=ones_b[:], rhs=yb_[:], start=(c == 0), stop=(c == 15)),
                               reads=[ybb_, B_const], writes=[psmb], signal=True)
                            op(PE, lambda c=c: nc.tensor.matmul(pss[:], lhsT=ones_b[:], rhs=ys_[:], start=(c == 0), stop=(c == 15)),
                               reads=[ysb_, B_const], writes=[pssb], signal=True)
                        op(DVE, lambda: nc.vector.tensor_scalar(out=mean[:], in0=psm[:], scalar1=1.0 / CC, scalar2=None, op0=ALU.mult), reads=[psmb], writes=[Bmean])
                        op(DVE, lambda: nc.vector.tensor_tensor(out=tmpc[:], in0=mean[:], in1=mean[:], op=ALU.mult), reads=[Bmean], writes=[Btmp])
                        op(DVE, lambda: nc.vector.scalar_tensor_tensor(out=tmpc[:], in0=pss[:], scalar=1.0 / CC, in1=tmpc[:], op0=ALU.mult, op1=ALU.subtract),
                           reads=[pssb, Btmp], writes=[Btmp])
                        op(DVE, lambda: nc.vector.tensor_scalar(out=tmpc[:], in0=tmpc[:], scalar1=EPS, scalar2=None, op0=ALU.add), reads=[Btmp], writes=[Btmp])
                        op(ACT, lambda: nc.scalar.activation(out=tmpc[:], in_=tmpc[:], func=AF.Sqrt), reads=[Btmp], writes=[Btmp])
                        op(DVE, lambda: nc.vector.reciprocal(rstd[:], tmpc[:]), reads=[Btmp], writes=[Brstd])
                        for c in range(16):
                            t_, tb_ = tnr.next()
                            op(DVE, lambda c=c: nc.vector.tensor_tensor(out=t_[:], in0=yc[:, c, :], in1=mean[:], op=ALU.subtract), reads=[Byc[c], Bmean], writes=[tb_])
                            op(DVE, lambda: nc.vector.tensor_tensor(out=t_[:], in0=t_[:], in1=rstd[:], op=ALU.mult), reads=[tb_, Brstd], writes=[tb_])
                            s_, sb_ = slr.next()
                            op(ACT, lambda c=c: nc.scalar.activation(out=s_[:], in_=t_[:], func=AF.Silu, scale=lngT[:, c:c + 1], bias=lnbT[:, c:c + 1]),
                               reads=[tb_, B_vec], writes=[sb_])
                            st, sbf, ss = ystr.next()
                            op(DVE, lambda c=c: nc.vector.tensor_tensor(out=st[:], in0=s_[:], in1=g_[:, c, :], op=ALU.mult), reads=[sb_, gb_], writes=[sbf])
                            dma(SP, ss, ycT_d[(16 + c) * 128:(17 + c) * 128, tb * 512:(tb + 1) * 512], st[:], reads=[sbf], writes=[dr("ycT", 16 + c, tb)])

                    scr = sb(ph, "scr", [CK - 1, CC], F32)
                    ucs = sb(ph, "ucs", [128, 16, CK], F32)
                    Bscr, Bucs = Buf(), Buf()
                    dma(SP, s_misc, scr[:], sc_in[l], writes=[Bscr])
                    for c4 in range(4):
                        pt, pb = psA.next()
                        for j in range(4):
                            c = c4 * 4 + j
                            op(PE, lambda c=c, j=j: nc.tensor.transpose(pt[:, j * 32:j * 32 + CK - 1], scr[:, c * 128:(c + 1) * 128], ident_f[0:CK - 1, 0:CK - 1]),
                               reads=[Bscr, B_const], writes=[pb])
                        op(DVE, lambda c4=c4: nc.vector.tensor_copy(ucs[:, c4 * 4:c4 * 4 + 4, 0:CK - 1], pt[:, 0:128].rearrange("p (j t) -> p j t", t=32)[:, :, 0:CK - 1]),
                           reads=[pb], writes=[Bucs])
                    op(DVE, lambda: nc.vector.tensor_copy(ucs[:, :, CK - 1:CK], us_f[:, 64:80].rearrange("p (c o) -> p c o", o=1)), reads=[B_us], writes=[Bucs])
                    op(DVE, lambda: nc.vector.tensor_tensor(out=ucs[:], in0=ucs[:], in1=cwT[:], op=ALU.mult), reads=[Bucs, B_cwT], writes=[Bucs])
                    op(DVE, lambda: nc.vector.reduce_sum(out=small[:, 0:16], in_=ucs[:], axis=AX.X), reads=[Bucs], writes=[B_small])
                    op(DVE, lambda: nc.vector.tensor_tensor(out=small[:, 0:16], in0=small[:, 0:16], in1=cbT, op=ALU.add), reads=[B_small, B_vec], writes=[B_small])
                    op(DVE, lambda: nc.vector.reduce_sum(out=small[:, 16:17], in_=small[:, 0:16], axis=AX.X), reads=[B_small], writes=[B_small])
                    op(DVE, lambda: nc.vector.tensor_tensor(out=small[:, 32:48], in0=small[:, 0:16], in1=small[:, 0:16], op=ALU.mult), reads=[B_small], writes=[B_small])
                    op(DVE, lambda: nc.vector.reduce_sum(out=small[:, 17:18], in_=small[:, 32:48], axis=AX.X), reads=[B_small], writes=[B_small])
                    pt, pb = psA.next()
                    op(PE, lambda: nc.tensor.matmul(pt[:, 0:2], lhsT=ones_f[:], rhs=small[:, 16:18], start=True, stop=True), reads=[B_small, B_const], writes=[pb])
                    op(DVE, lambda: nc.vector.tensor_scalar(out=small[:, 18:20], in0=pt[:, 0:2], scalar1=1.0 / CC, scalar2=None, op0=ALU.mult), reads=[pb], writes=[B_small])
                    op(DVE, lambda: nc.vector.tensor_tensor(out=small[:, 20:21], in0=small[:, 18:19], in1=small[:, 18:19], op=ALU.mult), reads=[B_small], writes=[B_small])
                    op(DVE, lambda: nc.vector.tensor_tensor(out=small[:, 20:21], in0=small[:, 19:20], in1=small[:, 20:21], op=ALU.subtract), reads=[B_small], writes=[B_small])
                    op(DVE, lambda: nc.vector.tensor_scalar(out=small[:, 20:21], in0=small[:, 20:21], scalar1=EPS, scalar2=None, op0=ALU.add), reads=[B_small], writes=[B_small])
                    op(ACT, lambda: nc.scalar.activation(out=small[:, 21:22], in_=small[:, 20:21], func=AF.Sqrt), reads=[B_small], writes=[B_small])
                    op(DVE, lambda: nc.vector.reciprocal(small[:, 22:23], small[:, 21:22]), reads=[B_small], writes=[B_small])
                    op(DVE, lambda: nc.vector.tensor_scalar(out=small[:, 0:16], in0=small[:, 0:16], scalar1=small[:, 18:19], scalar2=small[:, 22:23],
                                                            op0=ALU.subtract, op1=ALU.mult), reads=[B_small], writes=[B_small])
                    op(DVE, lambda: nc.vector.tensor_tensor(out=small[:, 0:16], in0=small[:, 0:16], in1=lngT, op=ALU.mult), reads=[B_small, B_vec], writes=[B_small])
                    op(DVE, lambda: nc.vector.tensor_tensor(out=small[:, 0:16], in0=small[:, 0:16], in1=lnbT, op=ALU.add), reads=[B_small, B_vec], writes=[B_small])
                    op(ACT, lambda: nc.scalar.activation(out=small[:, 32:48], in_=small[:, 0:16], func=AF.Silu), reads=[B_small], writes=[B_small])
                    op(ACT, lambda: nc.scalar.activation(out=small[:, 48:64], in_=us_f[:, 96:112], func=AF.Silu), reads=[B_us], writes=[B_small])
                    op(DVE, lambda: nc.vector.tensor_tensor(out=ycs_b[:, 16:32], in0=small[:, 32:48], in1=small[:, 48:64], op=ALU.mult), reads=[B_small], writes=[B_ycs])
                    barrier()
                    ck_auto()

                NPC = D // PW
                with ExitStack() as ph:
                    ssq = sb(ph, "ssq", [128, NT, NPC], F32)
                    Bssq = Buf()
                    rsy = sb(ph, "rsy", [128, NT], F32)
                    Brsy = Buf()
                    with ExitStack() as ph4:
                        ycT = sb(ph4, "ycT", [128, KC, S], BF16)
                        Byc_ = [Buf() for _ in range(KC)]
                        ycv = ycT_d.rearrange("(k p) t -> p k t", p=128)
                        for kq in range(8):
                            dma(SP, s_ld[kq % 4], ycT[:, kq * 4:(kq + 1) * 4, :], ycv[:, kq * 4:(kq + 1) * 4, :],
                                reads=[dr("ycT", kk, x) for kk in range(kq * 4, kq * 4 + 4) for x in range(4)], writes=Byc_[kq * 4:(kq + 1) * 4])
                        stf = [(sb(ph4, f"wstf{i}", [128, PW], F32), Buf(), s_st[i]) for i in range(3)]
                        stfr = Ring(stf)
                        junk = sb(ph4, "junk", [128, PW], BF16)
                        Bjunk = Buf()
                        op(DVE, lambda: nc.vector.memset(ssq[:], 0.0), writes=[Bssq])
                        wsrc = w_out[l]
                        pend = [load_piece(wsrc, [(0, PW, 0)])]
                        for pi in range(NPC):
                            if pi + 1 < NPC:
                                pend.append(load_piece(wsrc, [((pi + 1) * PW, PW, 0)]))
                            wt, wb = pend.pop(0)
                            for t in range(NT):
                                pt, pb = psA.next()
                                for k in range(KC):
                                    op(PE, lambda k=k: nc.tensor.matmul(pt[:, 0:PW], lhsT=ycT[:, k, t * 128:(t + 1) * 128], rhs=wt[:, k, :],
                                                                        start=(k == 0), stop=(k == KC - 1)),
                                       reads=[wb, Byc_[k]], writes=[pb], signal=(k == KC - 1))
                                st, sbf, ss = stfr.next()
                                op(ACT, lambda: nc.scalar.copy(out=st[:], in_=pt[:, 0:PW]), reads=[pb], writes=[sbf])
                                op(ACT, lambda t=t, pi=pi: nc.scalar.activation(out=junk[:], in_=st[:], func=AF.Square, accum_out=ssq[:, t, pi:pi + 1]),
                                   reads=[sbf], writes=[Bjunk, Bssq])
                                dma(SP, ss, ybuf[t * 128:(t + 1) * 128, pi * PW:(pi + 1) * PW], st[:], reads=[sbf], writes=[dr("ybuf", t, pi // 4)])
                            pt, pb = psB.next()
                            for j in range(2):
                                for k in range(KC):
                                    op(PE, lambda k=k, j=j: nc.tensor.matmul(pt[:, j:j + 1], lhsT=wt[:, k, j * 128:(j + 1) * 128], rhs=ycs_b[:, k:k + 1],
                                                                             start=(k == 0), stop=(k == KC - 1), skip_group_check=True),
                                       reads=[wb, B_ycs], writes=[pb], signal=(k == KC - 1))
                            op(DVE, lambda pi=pi: nc.vector.tensor_copy(ys_f[:, 2 * pi:2 * pi + 2], pt[:, 0:2]), reads=[pb], writes=[B_ys])
                        op(DVE, lambda: nc.vector.reduce_sum(out=rsy[:], in_=ssq[:], axis=AX.X), reads=[Bssq], writes=[Brsy])
                        op(DVE, lambda: nc.vector.tensor_scalar(out=rsy[:], in0=rsy[:], scalar1=1.0 / D, scalar2=EPS, op0=ALU.mult, op1=ALU.add), reads=[Brsy], writes=[Brsy])
                        op(ACT, lambda: nc.scalar.activation(out=rsy[:], in_=rsy[:], func=AF.Sqrt), reads=[Brsy], writes=[Brsy])
                        op(DVE, lambda: nc.vector.reciprocal(rsy[:], rsy[:]), reads=[Brsy], writes=[Brsy])
                        if _dbg:
                            dma(SP, s_misc, dbg_rsy[:, :], rsy[:], reads=[Brsy])
                        barrier()
                        ck_auto()

                    with ExitStack() as phv:
                        vrow = sb(phv, "vrow2", [128, 128], F32)
                        Bv = Buf()
                        op(DVE, lambda: nc.vector.memset(vrow[:], 0.0), writes=[Bv])
                        dma(SP, s_misc, vrow[0:32, :], g_post[l].rearrange("(k p) -> k p", p=128), writes=[Bv])
                        dma(SP, s_misc, vrow[32:64, :], g_ple[l].rearrange("(k p) -> k p", p=128), writes=[Bv])
                        dma(SP, s_misc, vrow[64:96, :], b_pg[l].rearrange("(k p) -> k p", p=128), writes=[Bv])
                        pt, pb = psA.next()
                        op(PE, lambda: nc.tensor.transpose(pt[:, 0:128], vrow[:], ident_f[:]), reads=[Bv, B_const], writes=[pb])
                        op(DVE, lambda: nc.vector.tensor_copy(vec[:, 4:7, :].rearrange("p a k -> p (a k)"), pt[:, 0:96]), reads=[pb], writes=[B_vec])
                        barrier()
                        ck_auto()
                    gpostT, gpleT, bpgT = vec[:, 4, :], vec[:, 5, :], vec[:, 6, :]

                    hmT = sb(ph, "hmT", [128, KC, S], BF16)
                    B_hm = [Buf() for _ in range(NT)]
                    with ExitStack() as ph5:
                        QW = 1024
                        gq = sb(ph5, "gq", [128, QW], F32)
                        Bgq = Buf()
                        yq = [(sb(ph5, f"yq{i}", [128, QW], F32), Buf(), s_ld[i]) for i in range(2)]
                        hq = [(sb(ph5, f"hq{i}", [128, QW], F32), Buf(), s_ld[2 + i]) for i in range(2)]
                        hb_ = [(sb(ph5, f"hbq{i}", [128, QW], BF16), Buf()) for i in range(2)]
                        yqr, hqr, hbr = Ring(yq), Ring(hq), Ring(hb_)
                        for cq in range(D // QW):
                            cs = slice(cq * QW, (cq + 1) * QW)
                            dma(SP, s_misc, gq[:], bass.AP(tensor=g_post.tensor, offset=g_post[l].offset + cq * QW, ap=[[0, 128], [1, QW]]), writes=[Bgq])
                            for t in range(NT):
                                y_, yb_, ysm = yqr.next()
                                h_, hbf_, hsm = hqr.next()
                                ts = slice(t * 128, (t + 1) * 128)
                                dma(SP, ysm, y_[:], ybuf[ts, cs], reads=[dr("ybuf", t, cq)], writes=[yb_])
                                dma(SP, hsm, h_[:], h_src[ts, cs], reads=[dr(hkey, t)], writes=[hbf_])
                                op(DVE, lambda t=t: nc.vector.scalar_tensor_tensor(out=y_[:], in0=y_[:], scalar=rsy[:, t:t + 1], in1=gq[:], op0=ALU.mult, op1=ALU.mult),
                                   reads=[yb_, Brsy, Bgq], writes=[yb_])
                                op(DVE, lambda: nc.vector.tensor_tensor(out=h_[:], in0=h_[:], in1=y_[:], op=ALU.add), reads=[hbf_, yb_], writes=[hbf_])
                                dma(SP, hsm, hmid[ts, cs], h_[:], reads=[hbf_], writes=[dr("hmid", t, cq)])
                                b16, b16b = hbr.next()
                                op(ACT, lambda: nc.scalar.copy(out=b16[:], in_=h_[:]), reads=[hbf_], writes=[b16b])
                                for k4 in range(QW // 512):
                                    pt, pb = psA.next()
                                    ptb = pt[:].bitcast(BF16)
                                    for j in range(4):
                                        op(PE, lambda k4=k4, j=j: nc.tensor.transpose(ptb[:, j * 128:(j + 1) * 128], b16[:, (k4 * 4 + j) * 128:(k4 * 4 + j + 1) * 128], ident_b[:]),
                                           reads=[b16b, B_const], writes=[pb], signal=(j == 3))
                                    kk = cq * (QW // 128) + k4 * 4
                                    op(ACT, lambda kk=kk, t=t: nc.scalar.copy(out=hmT[:, kk:kk + 4, t * 128:(t + 1) * 128], in_=ptb[:, 0:512].rearrange("p (j t) -> p j t", t=128)),
                                       reads=[pb], writes=[B_hm[t]])
                        op(DVE, lambda: nc.vector.tensor_tensor(out=small[:, 0:KC], in0=ys_f[:], in1=ys_f[:], op=ALU.mult), reads=[B_ys], writes=[B_small])
                        op(DVE, lambda: nc.vector.reduce_sum(out=small[:, 32:33], in_=small[:, 0:KC], axis=AX.X), reads=[B_small], writes=[B_small])
                        pt, pb = psA.next()
                        op(PE, lambda: nc.tensor.matmul(pt[:, 0:1], lhsT=ones_f[:], rhs=small[:, 32:33], start=True, stop=True), reads=[B_small, B_const], writes=[pb])
                        op(DVE, lambda: nc.vector.tensor_scalar(out=small[:, 33:34], in0=pt[:, 0:1], scalar1=1.0 / D, scalar2=EPS, op0=ALU.mult, op1=ALU.add), reads=[pb], writes=[B_small])
                        op(ACT, lambda: nc.scalar.activation(out=small[:, 34:35], in_=small[:, 33:34], func=AF.Sqrt), reads=[B_small], writes=[B_small])
                        op(DVE, lambda: nc.vector.reciprocal(small[:, 35:36], small[:, 34:35]), reads=[B_small], writes=[B_small])
                        op(DVE, lambda: nc.vector.scalar_tensor_tensor(out=small[:, 0:KC], in0=ys_f[:], scalar=small[:, 35:36], in1=gpostT, op0=ALU.mult, op1=ALU.mult),
                           reads=[B_ys, B_small, B_vec], writes=[B_small])
                        op(DVE, lambda: nc.vector.tensor_tensor(out=hs[:], in0=hs[:], in1=small[:, 0:KC], op=ALU.add), reads=[B_hs, B_small], writes=[B_hs])
                        op(DVE, lambda: nc.vector.tensor_copy(xs_b[:], hs[:]), reads=[B_hs], writes=[B_xs])
                        barrier()
                        ck_auto()

                    with ExitStack() as ph6:
                        pT = sb(ph6, "pT", [128, 2, S], BF16)
                        BpT = Buf()
                        wpl = sb(ph6, "wpl", [128, 2, D], BF16)
                        Bwpl = Buf()
                        s_wpl = s_misc2
                        for _f in engs:
                            if _f is not PL:
                                PL.wait(_f.sem, _f.count)
                        for _s in all_sems:
                            PL.wait(_s, _s.issued)
                        _deps(PL, (), [Bwpl])
                        PL.wait(s_wpl, s_wpl.issued)
                        _wv = w_ple[l].rearrange("(k p) c -> p k c", p=128)
                        for _c8 in range(D // PW):
                            nc.gpsimd.dma_start(out=wpl[:, :, _c8 * PW:(_c8 + 1) * PW], in_=_wv[:, :, _c8 * PW:(_c8 + 1) * PW]).then_inc(s_wpl.h, 16)
                            s_wpl.issued += 16
                        Bwpl.writers = {s_wpl: s_wpl.issued}
                        sse = sb(ph6, "sse", [128, NT, 8], F32)
                        rse = sb(ph6, "rse", [128, NT], F32)
                        Bsse, Brse = Buf(), Buf()
                        op(DVE, lambda: nc.vector.memset(sse[:], 0.0), writes=[Bsse])
                        pin = [(sb(ph6, f"pin{i}", [128, PLE], F32), Buf(), s_ld[i]) for i in range(1)]
                        pinr = Ring(pin)
                        pbf = [(sb(ph6, f"pbf{i}", [128, PLE], BF16), Buf()) for i in range(2)]
                        pbfr = Ring(pbf)
                        junk = sb(ph6, "junk6", [128, 512], BF16)
                        Bjunk = Buf()
                        for t in range(NT):
                            pi_, pib_, pis_ = pinr.next()
                            dma(SP, pis_, pi_[:], pp_in[l, t * 128:(t + 1) * 128, :], writes=[pib_])
                            pb_, pbb_ = pbfr.next()
                            op(ACT, lambda: nc.scalar.copy(out=pb_[:], in_=pi_[:]), reads=[pib_], writes=[pbb_])
                            pt, pb = psA.next()
                            ptb = pt[:].bitcast(BF16)
                            for j in range(2):
                                op(PE, lambda j=j: nc.tensor.transpose(ptb[:, j * 128:(j + 1) * 128], pb_[:, j * 128:(j + 1) * 128], ident_b[:]),
                                   reads=[pbb_, B_const], writes=[pb], signal=(j == 1))
                            op(DVE, lambda t=t: nc.vector.tensor_copy(pT[:, :, t * 128:(t + 1) * 128], ptb[:, 0:256].rearrange("p (j t) -> p j t", t=128)),
                               reads=[pb], writes=[BpT])
                        with nc.allow_non_contiguous_dma(reason="tiny feature-major vector loads"):
                            dma(SP, s_misc, small[:, 40:42], ps_in[l].rearrange("(k p) -> p k", p=128), writes=[B_small])
                        op(DVE, lambda: nc.vector.tensor_copy(pss_b[:], small[:, 40:42]), reads=[B_small], writes=[B_pss])
                        for t in range(NT):
                            for cg in range(8):
                                pt, pb = psA.next()
                                for j in range(2):
                                    op(PE, lambda j=j: nc.tensor.matmul(pt[:], lhsT=pT[:, j, t * 128:(t + 1) * 128], rhs=wpl[:, j, cg * 512:(cg + 1) * 512],
                                                                        start=(j == 0), stop=(j == 1)), reads=[BpT, Bwpl], writes=[pb], signal=(j == 1))
                                op(ACT, lambda t=t, cg=cg: nc.scalar.activation(out=junk[:], in_=pt[:], func=AF.Square, accum_out=sse[:, t, cg:cg + 1]),
                                   reads=[pb], writes=[Bjunk, Bsse])
                        op(DVE, lambda: nc.vector.reduce_sum(out=rse[:], in_=sse[:], axis=AX.X), reads=[Bsse], writes=[Brse])
                        op(DVE, lambda: nc.vector.tensor_scalar(out=rse[:], in0=rse[:], scalar1=1.0 / D, scalar2=EPS, op0=ALU.mult, op1=ALU.add), reads=[Brse], writes=[Brse])
                        op(ACT, lambda: nc.scalar.activation(out=rse[:], in_=rse[:], func=AF.Sqrt), reads=[Brse], writes=[Brse])
                        op(DVE, lambda: nc.vector.reciprocal(rse[:], rse[:]), reads=[Brse], writes=[Brse])
                        if _dbg:
                            dma(SP, s_misc, dbg_pT[:, :], pT[:].rearrange("p a b -> p (a b)"), reads=[BpT])
                            dma(SP, s_misc, dbg_wpl[:, :], wpl[:].rearrange("p a b -> p (a b)"), reads=[Bwpl])
                            dma(SP, s_misc, dbg_ssq[:, 0:128], sse[:].rearrange("p a b -> p (a b)"), reads=[Bsse])
                            dma(SP, s_misc, dbg_rsy2[:, :], rse[:], reads=[Brse])
                        for kq in range(8):
                            pt, pb = psB.next()
                            for j4 in range(4):
                                kk = kq * 4 + j4
                                for j in range(2):
                                    op(PE, lambda j=j, kk=kk, j4=j4: nc.tensor.matmul(pt[:, j4:j4 + 1], lhsT=wpl[:, j, kk * 128:(kk + 1) * 128], rhs=pss_b[:, j:j + 1],
                                                                                      start=(j == 0), stop=(j == 1), skip_group_check=True),
                                       reads=[Bwpl, B_pss], writes=[pb], signal=(j == 1))
                            op(DVE, lambda kq=kq: nc.vector.tensor_copy(es_f[:, kq * 4:kq * 4 + 4], pt[:, 0:4]), reads=[pb], writes=[B_es])

                        bcb = [(sb(ph6, f"bcb{i}", [128, PW], F32), Buf(), s_ld[2]) for i in range(2)]
                        bcg = [(sb(ph6, f"bcg{i}", [128, PW], F32), Buf(), s_ld[3]) for i in range(2)]
                        hml = [(sb(ph6, f"hml{i}", [128, PW], F32), Buf(), s_ld[4 + i]) for i in range(2)]
                        hmlr = Ring(hml)
                        zt = [(sb(ph6, f"zt{i}", [128, PW], F32), Buf()) for i in range(2)]
                        et = [(sb(ph6, f"et{i}", [128, PW], F32), Buf()) for i in range(2)]
                        ot = [(sb(ph6, f"ot{i}", [128, PW], F32), Buf(), s_st[i]) for i in range(2)]
                        ztr, etr, otr = Ring(zt), Ring(et), Ring(ot)
                        wsrc = w_pg[l]
                        pend = [load_piece(wsrc, [(0, PW, 0)])]
                        for pi in range(NPC):
                            if pi + 1 < NPC:
                                pend.append(load_piece(wsrc, [((pi + 1) * PW, PW, 0)]))
                            wt, wb = pend.pop(0)
                            bb_, bbb_, bbs_ = bcb[pi % 2]
                            gg_, ggb_, ggs_ = bcg[pi % 2]
                            dma(SP, bbs_, bb_[:], bass.AP(tensor=b_pg.tensor, offset=b_pg[l].offset + pi * PW, ap=[[0, 128], [1, PW]]), writes=[bbb_])
                            dma(SP, ggs_, gg_[:], bass.AP(tensor=g_ple.tensor, offset=g_ple[l].offset + pi * PW, ap=[[0, 128], [1, PW]]), writes=[ggb_])
                            for t in range(NT):
                                ts = slice(t * 128, (t + 1) * 128)
                                cs = slice(pi * PW, (pi + 1) * PW)
                                hm_, hmb_, hms_ = hmlr.next()
                                dma(SP, hms_, hm_[:], hmid[ts, cs], reads=[dr("hmid", t, pi // 4)], writes=[hmb_])
                                pt, pb = psA.next()
                                for k in range(KC):
                                    op(PE, lambda k=k: nc.tensor.matmul(pt[:, 0:PW], lhsT=hmT[:, k, ts], rhs=wt[:, k, :], start=(k == 0), stop=(k == KC - 1)),
                                       reads=[wb, B_hm[t]], writes=[pb], signal=(k == KC - 1))
                                for j in range(2):
                                    op(PE, lambda j=j: nc.tensor.matmul(pt[:, PW:2 * PW], lhsT=pT[:, j, ts], rhs=wpl[:, j, cs], start=(j == 0), stop=(j == 1), skip_group_check=True),
                                       reads=[BpT, Bwpl], writes=[pb], signal=(j == 1))
                                z_, zb_ = ztr.next()
                                e_, eb_ = etr.next()
                                o_, ob_, os_ = otr.next()
                                op(DVE, lambda: nc.vector.tensor_tensor(out=z_[:], in0=pt[:, 0:PW], in1=bb_[:], op=ALU.add), reads=[pb, bbb_], writes=[zb_])
                                op(ACT, lambda: nc.scalar.activation(out=z_[:], in_=z_[:], func=AF.Sigmoid), reads=[zb_], writes=[zb_])
                                op(DVE, lambda t=t: nc.vector.scalar_tensor_tensor(out=e_[:], in0=pt[:, PW:2 * PW], scalar=rse[:, t:t + 1], in1=gg_[:], op0=ALU.mult, op1=ALU.mult),
                                   reads=[pb, Brse, ggb_], writes=[eb_])
                                op(DVE, lambda: nc.vector.tensor_tensor(out=e_[:], in0=e_[:], in1=z_[:], op=ALU.mult), reads=[eb_, zb_], writes=[eb_])
                                op(DVE, lambda: nc.vector.tensor_tensor(out=o_[:], in0=e_[:], in1=hm_[:], op=ALU.add), reads=[eb_, hmb_], writes=[ob_])
                                dma(SP, os_, h_dst[ts, cs], o_[:], reads=[ob_], writes=[dr(hdkey, t)])
                            pt, pb = psB.next()
                            for j in range(2):
                                for k in range(KC):
                                    op(PE, lambda k=k, j=j: nc.tensor.matmul(pt[:, j:j + 1], lhsT=wt[:, k, j * 128:(j + 1) * 128], rhs=xs_b[:, k:k + 1],
                                                                             start=(k == 0), stop=(k == KC - 1), skip_group_check=True),
                                       reads=[wb, B_xs], writes=[pb], signal=(k == KC - 1))
                            op(DVE, lambda pi=pi: nc.vector.tensor_copy(ys_f[:, 2 * pi:2 * pi + 2], pt[:, 0:2]), reads=[pb], writes=[B_ys])
                        op(DVE, lambda: nc.vector.tensor_tensor(out=ys_f[:], in0=ys_f[:], in1=bpgT, op=ALU.add), reads=[B_ys, B_vec], writes=[B_ys])
                        op(ACT, lambda: nc.scalar.activation(out=ys_f[:], in_=ys_f[:], func=AF.Sigmoid), reads=[B_ys], writes=[B_ys])
                        op(DVE, lambda: nc.vector.tensor_tensor(out=small[:, 0:KC], in0=es_f[:], in1=es_f[:], op=ALU.mult), reads=[B_es], writes=[B_small])
                        op(DVE, lambda: nc.vector.reduce_sum(out=small[:, 32:33], in_=small[:, 0:KC], axis=AX.X), reads=[B_small], writes=[B_small])
                        pt, pb = psA.next()
                        op(PE, lambda: nc.tensor.matmul(pt[:, 0:1], lhsT=ones_f[:], rhs=small[:, 32:33], start=True, stop=True), reads=[B_small, B_const], writes=[pb])
                        op(DVE, lambda: nc.vector.tensor_scalar(out=small[:, 33:34], in0=pt[:, 0:1], scalar1=1.0 / D, scalar2=EPS, op0=ALU.mult, op1=ALU.add), reads=[pb], writes=[B_small])
                        op(ACT, lambda: nc.scalar.activation(out=small[:, 34:35], in_=small[:, 33:34], func=AF.Sqrt), reads=[B_small], writes=[B_small])
                        op(DVE, lambda: nc.vector.reciprocal(small[:, 35:36], small[:, 34:35]), reads=[B_small], writes=[B_small])
                        op(DVE, lambda: nc.vector.scalar_tensor_tensor(out=small[:, 0:KC], in0=es_f[:], scalar=small[:, 35:36], in1=gpleT, op0=ALU.mult, op1=ALU.mult),
                           reads=[B_es, B_small, B_vec], writes=[B_small])
                        op(DVE, lambda: nc.vector.tensor_tensor(out=small[:, 0:KC], in0=small[:, 0:KC], in1=ys_f[:], op=ALU.mult), reads=[B_small, B_ys], writes=[B_small])
                        op(DVE, lambda: nc.vector.tensor_tensor(out=hs[:], in0=hs[:], in1=small[:, 0:KC], op=ALU.add), reads=[B_hs, B_small], writes=[B_hs])
                        barrier()
                        ck_auto()
        except _Stop:
            pass
        _ST[0] = False

        hsr = sb(gs, "hsr", [KC, 128], F32)
        Bhsr = Buf()
        ptf_, pbf_ = psA.next()
        op(PE, lambda: nc.tensor.transpose(ptf_[0:KC, 0:128], hs[:], ident_f[:]), reads=[B_hs, B_const], writes=[pbf_])
        op(DVE, lambda: nc.vector.tensor_copy(hsr[:], ptf_[0:KC, 0:128]), reads=[pbf_], writes=[Bhsr])
        dma(SP, s_misc, y_s.rearrange("(k p) -> k p", p=128), hsr[:], reads=[Bhsr], writes=[dr("ys")])
        barrier(include_pool=True)
    return nc


_CACHE = {}


def kernel(x_prompt, x_sample, cache_k, cache_v, state_conv, p_prompt, p_sample, rel_bias,
           g_pre, w_in, conv_w, conv_b, ln_g, ln_b, w_out, g_post, w_ple, g_ple, w_pg, b_pg):
    f = lambda a: np.ascontiguousarray(np.asarray(a, dtype=np.float32))
    x_prompt, x_sample, cache_k, cache_v, state_conv = map(f, (x_prompt, x_sample, cache_k, cache_v, state_conv))
    p_prompt, p_sample = f(p_prompt), f(p_sample)
    shared = dict(rel_bias=f(rel_bias), g_pre=f(g_pre), w_in=f(w_in), conv_w=f(conv_w), conv_b=f(conv_b), ln_g=f(ln_g), ln_b=f(ln_b),
                  w_out=f(w_out), g_post=f(g_post), w_ple=f(w_ple), g_ple=f(g_ple), w_pg=f(w_pg), b_pg=f(b_pg))
    oh, ohs = _consts()
    shared["oh_c"] = oh.reshape(NB, 3 * TL)
    shared["ohs_c"] = ohs.reshape(NB, 3 * 128)
    shared["ident_c"] = np.eye(128, dtype=np.float32)
    if "nc" not in _CACHE:
        _CACHE["nc"] = build_nc()
    nc = _CACHE["nc"]
    in_maps = []
    for c in range(8):
        b = c % 4
        m = dict(shared)
        m["x_p"] = x_prompt[b]
        m["x_s"] = x_sample[c, 0]
        m["ck"] = np.ascontiguousarray(cache_k[:, c].reshape(DEPTH, LW, ATT))
        m["cv"] = np.ascontiguousarray(cache_v[:, c].reshape(DEPTH, LW, ATT))
        m["sc"] = np.ascontiguousarray(state_conv[:, c])
        m["pp"] = np.ascontiguousarray(p_prompt[:, b])
        m["psm"] = np.ascontiguousarray(p_sample[:, c, 0])
        in_maps.append(m)
    res = run_bass_kernel_spmd(nc, in_maps, core_ids=list(range(8)))
    R = res.results
    y_prompt = np.stack([R[b]["y_p"] for b in range(4)], 0)
    y_sample = np.stack([R[c]["y_s"] for c in range(8)], 0).reshape(8, 1, D)
    nk_p = np.stack([R[b]["nk_p"] for b in range(4)], 1).reshape(DEPTH, 4, S, H, HD)
    nv_p = np.stack([R[b]["nv_p"] for b in range(4)], 1).reshape(DEPTH, 4, S, H, HD)
    nc_p = np.stack([R[b]["nc_p"] for b in range(4)], 1)
    nk_s = np.stack([R[c]["nk_s"] for c in range(8)], 1).reshape(DEPTH, 8, LW, H, HD)
    nv_s = np.stack([R[c]["nv_s"] for c in range(8)], 1).reshape(DEPTH, 8, LW, H, HD)
    nc_s = np.stack([R[c]["nc_s"] for c in range(8)], 1)
    return (y_prompt, y_sample, nk_p, nv_p, nc_p, nk_s, nv_s, nc_s)
```

```python
import math
from contextlib import ExitStack
import numpy as np
import concourse.bass as bass
import concourse.mybir as mybir
from concourse.bass_utils import run_bass_kernel_spmd

F32 = mybir.dt.float32
BF16 = mybir.dt.bfloat16
AF = mybir.ActivationFunctionType
ALU = mybir.AluOpType
AX = mybir.AxisListType

D = 4096
S = 2048
NT = S // 128
DEPTH = 2
H = 16
HD = 128
ATT = 2048
CC = 2048
INC = 14336
KC = D // 128
PLE = 256
CK = 31
LW = 2048
EPS = 1e-6
PW = 256
DILS = (1, 4, 16)
TL = 383
NB = 32


def _bucket(dist):
    max_exact = NB // 2
    df = np.maximum(dist, 1).astype(np.float32)
    large = max_exact + (np.log(df / max_exact) / math.log(2048 / max_exact) * (NB - max_exact)).astype(np.int32)
    large = np.minimum(large, NB - 1)
    return np.where(dist < max_exact, dist, large)


def _consts():
    oh = np.zeros((NB, 3, TL), np.float32)
    ohs = np.zeros((NB, 3, 128), np.float32)
    for p, dil in enumerate(DILS):
        for x in range(TL):
            delta = x - 127
            if 0 <= delta <= 128:
                oh[int(_bucket(np.array(delta * dil))), p, x] = 1.0
        for jj in range(128):
            dist = (128 - jj) * dil
            ohs[int(_bucket(np.array(dist))), p, jj] = 1.0
    return oh, ohs


class _Stop(Exception):
    pass


import os as _os
KSTOP = _os.environ.get("KSTOP", "")


_CK = [0]


_ST = [False]


def ck_auto():
    _CK[0] += 1
    if KSTOP and int(KSTOP) == _CK[0]:
        _ST[0] = True


class Sem:
    def __init__(self, nc, name):
        self.h = nc.alloc_semaphore(name)
        self.issued = 0


class Buf:
    __slots__ = ("writers", "readers")

    def __init__(self):
        self.writers = {}
        self.readers = {}


class EngQ:
    def __init__(self, nc, eng, name):
        self.eng = eng
        self.sem = Sem(nc, "e_" + name)
        self.count = 0
        self.pending = False
        self.waited = {}
        self.is_pe = (name == "pe")

    def wait(self, sem, val):
        if val <= 0:
            return
        if sem is self.sem:
            if self.is_pe or val > self.count:
                return
        if self.waited.get(sem, 0) >= val:
            return
        self.eng.wait_ge(sem.h, val)
        self.waited[sem] = val


def _deps(E, reads, writes):
    for b in reads:
        for s, v in b.writers.items():
            E.wait(s, v)
    for b in writes:
        for s, v in b.writers.items():
            E.wait(s, v)
        for s, v in b.readers.items():
            E.wait(s, v)


def op(E, fn, reads=(), writes=(), signal=True):
    if _ST[0]:
        return None
    _deps(E, reads, writes)
    inst = fn()
    cid = E.count + 1
    if signal:
        inst.then_inc(E.sem.h, 1)
        E.count = cid
        E.pending = False
    else:
        E.pending = True
    for b in writes:
        b.writers = {E.sem: cid}
        b.readers = {}
    for b in reads:
        if b.readers.get(E.sem, 0) < cid:
            b.readers[E.sem] = cid
    return inst


def dma(Q, sem, out, in_, reads=(), writes=()):
    if _ST[0]:
        return None
    _deps(Q, reads, writes)
    Q.wait(sem, sem.issued)
    inst = Q.eng.dma_start(out=out, in_=in_)
    inst.then_inc(sem.h, 16)
    sem.issued += 16
    for b in writes:
        b.writers = {sem: sem.issued}
        b.readers = {}
    for b in reads:
        b.readers[sem] = sem.issued
    return inst


class Ring:
    def __init__(self, items):
        self.items = items
        self.i = 0

    def next(self):
        it = self.items[self.i % len(self.items)]
        self.i += 1
        return it


def rap(base, off, dims):
    return bass.AP(tensor=base.tensor, offset=base.offset + off, ap=[list(base.ap[0])] + [list(d) for d in dims])


def build_nc():
    nc = bass.Bass("TRN2", target_bir_lowering=False)
    dt_in = lambda n, s, d=F32: nc.dram_tensor(n, list(s), d, kind="ExternalInput").ap()
    dt_out = lambda n, s, d=F32: nc.dram_tensor(n, list(s), d, kind="ExternalOutput").ap()
    _dbg = _os.environ.get("KDBG", "") == "1"
    dt_scr = lambda n, s, d=F32: nc.dram_tensor(n, list(s), d, kind=("ExternalOutput" if (_dbg and n in ("hbuf", "ybuf", "hmid", "ycT_d")) else "Internal")).ap()

    x_p = dt_in("x_p", [S, D])
    x_s = dt_in("x_s", [D])
    ck_in = dt_in("ck", [DEPTH, LW, ATT])
    cv_in = dt_in("cv", [DEPTH, LW, ATT])
    sc_in = dt_in("sc", [DEPTH, CK - 1, CC])
    pp_in = dt_in("pp", [DEPTH, S, PLE])
    ps_in = dt_in("psm", [DEPTH, PLE])
    relb = dt_in("rel_bias", [NB, H])
    g_pre = dt_in("g_pre", [DEPTH, D])
    w_in = dt_in("w_in", [DEPTH, D, INC])
    conv_w = dt_in("conv_w", [DEPTH, CK, CC])
    conv_b = dt_in("conv_b", [DEPTH, CC])
    ln_g = dt_in("ln_g", [DEPTH, CC])
    ln_b = dt_in("ln_b", [DEPTH, CC])
    w_out = dt_in("w_out", [DEPTH, D, D])
    g_post = dt_in("g_post", [DEPTH, D])
    w_ple = dt_in("w_ple", [DEPTH, PLE, D])
    g_ple = dt_in("g_ple", [DEPTH, D])
    w_pg = dt_in("w_pg", [DEPTH, D, D])
    b_pg = dt_in("b_pg", [DEPTH, D])
    oh_c = dt_in("oh_c", [NB, 3 * TL])
    ohs_c = dt_in("ohs_c", [NB, 3 * 128])
    ident_c = dt_in("ident_c", [128, 128])

    y_p = dt_out("y_p", [S, D])
    y_s = dt_out("y_s", [D])
    nk_p = dt_out("nk_p", [DEPTH, S, ATT])
    nv_p = dt_out("nv_p", [DEPTH, S, ATT])
    nc_p = dt_out("nc_p", [DEPTH, CK - 1, CC])
    nk_s = dt_out("nk_s", [DEPTH, LW, ATT])
    nv_s = dt_out("nv_s", [DEPTH, LW, ATT])
    nc_s = dt_out("nc_s", [DEPTH, CK - 1, CC])

    if _dbg:
        dbg_ssq = dt_out("dbg_ssq", [128, 256])
        dbg_rsy = dt_out("dbg_rsy", [128, 16])
        dbg_rsy2 = dt_out("dbg_rsy2", [128, 16])
        dbg_pT = dt_out("dbg_pT", [128, 2 * S], BF16)
        dbg_wpl = dt_out("dbg_wpl", [128, 2 * D], BF16)
    hbuf = dt_scr("hbuf", [S, D])
    ybuf = dt_scr("ybuf", [S, D])
    hmid = dt_scr("hmid", [S, D])
    qT_d = dt_scr("qT_d", [ATT, S], BF16)
    kT_d = dt_scr("kT_d", [ATT, S], BF16)
    gaT_d = dt_scr("gaT_d", [ATT, S], BF16)
    uT_d = dt_scr("uT_d", [CC, S], BF16)
    gbT_d = dt_scr("gbT_d", [CC, S], BF16)
    v_d = dt_scr("v_d", [S, ATT], BF16)
    ycT_d = dt_scr("ycT_d", [D, S], BF16)
    R_d = dt_scr("R_d", [H, 128, 3 * TL])

    PE = EngQ(nc, nc.tensor, "pe")
    ACT = EngQ(nc, nc.scalar, "act")
    DVE = EngQ(nc, nc.vector, "dve")
    PL = EngQ(nc, nc.gpsimd, "pool")
    SP = EngQ(nc, nc.sync, "sp")
    engs = [PE, ACT, DVE, PL, SP]
    all_sems = []

    def newsem(name):
        s = Sem(nc, name)
        all_sems.append(s)
        return s

    dregs = {}

    def dr(*key):
        b = dregs.get(key)
        if b is None:
            b = Buf()
            dregs[key] = b
        return b

    def barrier(include_pool=False):
        if _ST[0]:
            return
        es = [e for e in engs if include_pool or e is not PL]
        for e in es:
            for f in engs:
                if f is not e:
                    e.wait(f.sem, f.count)
            for s in all_sems:
                e.wait(s, s.issued)

    with ExitStack() as gs:
        _uid = [0]

        def sb(st, name, shape, dt):
            _uid[0] += 1
            return st.enter_context(nc.sbuf_tensor(f"{name}_{_uid[0]}", list(shape), dt))
        wbufs = []
        for i in range(2):
            wbufs.append((sb(gs, f"wbuf{i}", [128, KC, PW], BF16), Buf(), newsem(f"s_w{i}")))
        wring = Ring(wbufs)
        ident_b = sb(gs, "ident_b", [128, 128], BF16)
        ident_f = sb(gs, "ident_f", [128, 128], F32)
        ones_b = sb(gs, "ones_b", [128, 128], BF16)
        ones_f = sb(gs, "ones_f", [128, 128], F32)
        sel0 = sb(gs, "sel0", [NB, 128], F32)
        Eexp = sb(gs, "Eexp", [NB, H], F32)
        E0bc = sb(gs, "E0bc", [128, H], F32)
        EBs = sb(gs, "EBs", [128, 3, H], F32)
        hs = sb(gs, "hs", [128, KC], F32)
        xs_b = sb(gs, "xs_b", [128, KC], BF16)
        us_f = sb(gs, "us_f", [128, 112], F32)
        ycs_b = sb(gs, "ycs_b", [128, KC], BF16)
        ys_f = sb(gs, "ys_f", [128, KC], F32)
        es_f = sb(gs, "es_f", [128, KC], F32)
        pss_b = sb(gs, "pss_b", [128, 2], BF16)
        vec = sb(gs, "vec", [128, 8, KC], F32)
        cwT = sb(gs, "cwT", [128, 16, CK], F32)
        small = sb(gs, "small", [128, 64], F32)
        B_const = Buf()
        B_hs, B_xs, B_us, B_ycs, B_ys, B_es, B_pss, B_vec, B_cwT, B_small = (Buf() for _ in range(10))
        s_misc = newsem("s_misc")
        s_misc2 = newsem("s_misc2")
        s_st = [newsem(f"s_st{i}") for i in range(6)]
        s_ld = [newsem(f"s_ld{i}") for i in range(8)]

        psum = []
        for i in range(8):
            psum.append((gs.enter_context(nc.psum_tensor(f"ps{i}", [128, 512], F32)), Buf()))
        psA = Ring(psum[0:4])
        psB = Ring(psum[4:6])
        psC = Ring(psum[6:8])

        _CK[0] = 0
        _ST[0] = False
        try:
            op(DVE, lambda: nc.vector.memset(ones_b[:], 1.0), writes=[B_const])
            op(DVE, lambda: nc.vector.memset(ones_f[:], 1.0), writes=[B_const])
            op(DVE, lambda: nc.vector.memset(sel0[:], 0.0), writes=[B_const])
            op(DVE, lambda: nc.vector.memset(sel0[0:1, :], 1.0), writes=[B_const])
            dma(SP, s_misc, ident_f[:], ident_c[:, :], writes=[B_const])
            op(DVE, lambda: nc.vector.tensor_copy(ident_b[:], ident_f[:]), reads=[B_const], writes=[B_const])

            def load_piece(wsrc, col_ranges):
                t, b, s = wring.next()
                wv = wsrc.rearrange("(k p) c -> p k c", p=128)
                _deps(PL, (), [b])
                PL.wait(s, s.issued)
                for (c0, n, o) in col_ranges:
                    inst = nc.gpsimd.dma_start(out=t[:, :, o:o + n], in_=wv[:, :, c0:c0 + n])
                    inst.then_inc(s.h, 16)
                    s.issued += 16
                b.writers = {s: s.issued}
                b.readers = {}
                return t, b

            with ExitStack() as ph:
                rb = sb(ph, "rb", [NB, H], F32)
                ohc = sb(ph, "ohc", [NB, 3 * TL], F32)
                ohsc = sb(ph, "ohsc", [NB, 3 * 128], F32)
                Ebh = sb(ph, "Ebh", [NB, 128], F32)
                Rsb = sb(ph, "Rsb", [128, 3 * TL], F32)
                Bt, BE, BR = Buf(), Buf(), Buf()
                dma(SP, s_misc, rb[:], relb[:, :], writes=[Bt])
                dma(SP, s_misc, ohc[:], oh_c[:, :], writes=[Bt])
                dma(SP, s_misc, ohsc[:], ohs_c[:, :], writes=[Bt])
                op(ACT, lambda: nc.scalar.activation(out=Eexp[:], in_=rb[:], func=AF.Exp), reads=[Bt], writes=[B_const])
                pt, pb = psA.next()
                for p in range(3):
                    op(PE, lambda p=p: nc.tensor.matmul(pt[:, p * H:(p + 1) * H], lhsT=ohsc[:, p * 128:(p + 1) * 128], rhs=Eexp[:],
                                                        start=True, stop=True), reads=[Bt, B_const], writes=[pb])
                op(PE, lambda: nc.tensor.matmul(pt[:, 64:64 + H], lhsT=sel0[:], rhs=Eexp[:], start=True, stop=True),
                   reads=[B_const], writes=[pb])
                op(DVE, lambda: nc.vector.tensor_copy(EBs[:].rearrange("p a h -> p (a h)"), pt[:, 0:3 * H]), reads=[pb], writes=[B_const])
                op(DVE, lambda: nc.vector.tensor_copy(E0bc[:], pt[:, 64:64 + H]), reads=[pb], writes=[B_const])
                for h in range(H):
                    op(DVE, lambda h=h: nc.vector.tensor_copy(Ebh[:], rap(Eexp[:], h, [[0, 128]])), reads=[B_const], writes=[BE])
                    pts = [psA.next() for _ in range(3)]
                    for p in range(3):
                        op(PE, lambda p=p: nc.tensor.matmul(pts[p][0][:, 0:TL], lhsT=Ebh[:], rhs=ohc[:, p * TL:(p + 1) * TL],
                                                            start=True, stop=True), reads=[BE, Bt], writes=[pts[p][1]])
                    for p in range(3):
                        op(ACT, lambda p=p: nc.scalar.copy(out=Rsb[:, p * TL:(p + 1) * TL], in_=pts[p][0][:, 0:TL]),
                           reads=[pts[p][1]], writes=[BR])
                    dma(SP, s_misc2, R_d[h], Rsb[:], reads=[BR], writes=[dr("R", h)])
                barrier()
                ck_auto()

            with nc.allow_non_contiguous_dma(reason="tiny feature-major vector loads"):
                dma(SP, s_misc, hs[:], x_s.rearrange("(k p) -> p k", p=128), writes=[B_hs])
            for l in range(DEPTH):
                for (src, dst) in ((ck_in, nk_s), (cv_in, nv_s)):
                    dma(ACT, newsem(f"s_bk{l}_{id(dst) % 1000}a"), dst[l, 0:2032, :], src[l, 1:2033, :], writes=[dr("nks", id(dst), l, 0)])
                    dma(ACT, newsem(f"s_bk{l}_{id(dst) % 1000}b"), dst[l, 2032:2047, :], src[l, 2033:2048, :], writes=[dr("nks", id(dst), l, 1)])
                dma(ACT, newsem(f"s_bk{l}c"), nc_s[l, 0:CK - 2, :], sc_in[l, 1:CK - 1, :], writes=[dr("ncs", l)])

            for l in range(DEPTH):
                h_src = x_p if l == 0 else hbuf
                h_dst = hbuf if l == 0 else y_p
                hkey = "x" if l == 0 else "hbuf"
                hdkey = "hbuf" if l == 0 else "yp"

                with ExitStack() as ph:
                    vrow = sb(ph, "vrow", [128, 128], F32)
                    cwr = sb(ph, "cwr", [CK, CC], F32)
                    Bv, Bc = Buf(), Buf()
                    op(DVE, lambda: nc.vector.memset(vrow[:], 0.0), writes=[Bv])
                    dma(SP, s_misc, vrow[0:32, :], g_pre[l].rearrange("(k p) -> k p", p=128), writes=[Bv])
                    dma(SP, s_misc, vrow[32:48, :], conv_b[l].rearrange("(k p) -> k p", p=128), writes=[Bv])
                    dma(SP, s_misc, vrow[48:64, :], ln_g[l].rearrange("(k p) -> k p", p=128), writes=[Bv])
                    dma(SP, s_misc, vrow[64:80, :], ln_b[l].rearrange("(k p) -> k p", p=128), writes=[Bv])
                    dma(SP, s_misc2, cwr[:], conv_w[l], writes=[Bc])
                    pt, pb = psA.next()
                    op(PE, lambda: nc.tensor.transpose(pt[:, 0:128], vrow[:], ident_f[:]), reads=[Bv, B_const], writes=[pb])
                    op(DVE, lambda: nc.vector.tensor_copy(vec[:, 0, :], pt[:, 0:32]), reads=[pb], writes=[B_vec])
                    op(DVE, lambda: nc.vector.tensor_copy(vec[:, 1, 0:16], pt[:, 32:48]), reads=[pb], writes=[B_vec])
                    op(DVE, lambda: nc.vector.tensor_copy(vec[:, 2, 0:16], pt[:, 48:64]), reads=[pb], writes=[B_vec])
                    op(DVE, lambda: nc.vector.tensor_copy(vec[:, 3, 0:16], pt[:, 64:80]), reads=[pb], writes=[B_vec])
                    for c4 in range(4):
                        pt, pb = psA.next()
                        for j in range(4):
                            c = c4 * 4 + j
                            op(PE, lambda c=c, j=j: nc.tensor.transpose(pt[:, j * 32:j * 32 + CK], cwr[:, c * 128:(c + 1) * 128], ident_f[0:CK, 0:CK]),
                               reads=[Bc, B_const], writes=[pb])
                        op(DVE, lambda c4=c4: nc.vector.tensor_copy(cwT[:, c4 * 4:c4 * 4 + 4, :], pt[:, 0:128].rearrange("p (j t) -> p j t", t=32)[:, :, 0:CK]),
                           reads=[pb], writes=[B_cwT])
                    barrier()
                    ck_auto()
                gpreT = vec[:, 0, :]
                cbT = vec[:, 1, 0:16]
                lngT = vec[:, 2, 0:16]
                lnbT = vec[:, 3, 0:16]

                with ExitStack() as ph:
                    xnT = sb(ph, "xnT", [128, KC, S], BF16)
                    B_xn = [Buf() for _ in range(NT)]
                    with ExitStack() as ph0:
                        ht = sb(ph0, "ht", [128, D], F32)
                        xb = sb(ph0, "xb", [128, D], BF16)
                        st0 = sb(ph0, "st0", [128, 8], F32)
                        Bh, Bxb, Bst = Buf(), Buf(), Buf()
                        for t in range(NT):
                            dma(SP, s_ld[0], ht[:], h_src[t * 128:(t + 1) * 128, :], reads=[dr(hkey, t)], writes=[Bh])
                            op(ACT, lambda: nc.scalar.activation(out=xb[:], in_=ht[:], func=AF.Square, accum_out=st0[:, 0:1]),
                               reads=[Bh], writes=[Bxb, Bst])
                            op(DVE, lambda: nc.vector.tensor_scalar(out=st0[:, 1:2], in0=st0[:, 0:1], scalar1=1.0 / D, scalar2=EPS,
                                                                    op0=ALU.mult, op1=ALU.add), reads=[Bst], writes=[Bst])
                            op(ACT, lambda: nc.scalar.activation(out=st0[:, 2:3], in_=st0[:, 1:2], func=AF.Sqrt), reads=[Bst], writes=[Bst])
                            op(DVE, lambda: nc.vector.reciprocal(st0[:, 3:4], st0[:, 2:3]), reads=[Bst], writes=[Bst])
                            op(ACT, lambda: nc.scalar.activation(out=xb[:], in_=ht[:], func=AF.Copy, scale=st0[:, 3:4]),
                               reads=[Bh, Bst], writes=[Bxb])
                            for k4 in range(KC // 4):
                                pt, pb = psA.next()
                                ptb = pt[:].bitcast(BF16)
                                for j in range(4):
                                    k = k4 * 4 + j
                                    op(PE, lambda k=k, j=j: nc.tensor.transpose(ptb[:, j * 128:(j + 1) * 128], xb[:, k * 128:(k + 1) * 128], ident_b[:]),
                                       reads=[Bxb, B_const], writes=[pb], signal=(j == 3))
                                op(DVE, lambda k4=k4, t=t: nc.vector.tensor_tensor(
                                    out=xnT[:, k4 * 4:k4 * 4 + 4, t * 128:(t + 1) * 128],
                                    in0=ptb[:, 0:512].rearrange("p (j t) -> p j t", t=128),
                                    in1=rap(gpreT, k4 * 4, [[1, 4], [0, 128]]), op=ALU.mult),
                                   reads=[pb, B_vec], writes=[B_xn[t]])
                        op(DVE, lambda: nc.vector.tensor_tensor(out=small[:, 0:KC], in0=hs[:], in1=hs[:], op=ALU.mult), reads=[B_hs], writes=[B_small])
                        op(DVE, lambda: nc.vector.reduce_sum(out=small[:, 32:33], in_=small[:, 0:KC], axis=AX.X), reads=[B_small], writes=[B_small])
                        pt, pb = psA.next()
                        op(PE, lambda: nc.tensor.matmul(pt[:, 0:1], lhsT=ones_f[:], rhs=small[:, 32:33], start=True, stop=True),
                           reads=[B_small, B_const], writes=[pb])
                        op(DVE, lambda: nc.vector.tensor_scalar(out=small[:, 33:34], in0=pt[:, 0:1], scalar1=1.0 / D, scalar2=EPS,
                                                                op0=ALU.mult, op1=ALU.add), reads=[pb], writes=[B_small])
                        op(ACT, lambda: nc.scalar.activation(out=small[:, 34:35], in_=small[:, 33:34], func=AF.Sqrt), reads=[B_small], writes=[B_small])
                        op(DVE, lambda: nc.vector.reciprocal(small[:, 35:36], small[:, 34:35]), reads=[B_small], writes=[B_small])
                        op(DVE, lambda: nc.vector.scalar_tensor_tensor(out=xs_b[:], in0=hs[:], scalar=small[:, 35:36], in1=gpreT,
                                                                       op0=ALU.mult, op1=ALU.mult), reads=[B_hs, B_small, B_vec], writes=[B_xs])
                        barrier()
                        ck_auto()

                    with ExitStack() as ph1:
                        stb = [(sb(ph1, f"stb{i}", [128, 512], BF16), Buf(), s_st[i]) for i in range(2)]
                        stf = [(sb(ph1, f"stf{i}", [128, 512], F32), Buf(), s_st[2 + i]) for i in range(2)]
                        stbr, stfr = Ring(stb), Ring(stf)
                        glu = sb(ph1, "glu", [128, S], F32)
                        Bglu = [Buf() for _ in range(4)]
                        ufp = sb(ph1, "ufp", [128, 512], F32)
                        Bufp = Buf()
                        sig = sb(ph1, "sig", [128, 512], F32)
                        Bsig = Buf()
                        cst = sb(ph1, "cst", [32, CC], F32)
                        Bcst = Buf()

                        def fm_chunk(wt, wb, off, kind, ci):
                            for tb in range(4):
                                pt, pb = psA.next()
                                for k in range(KC):
                                    op(PE, lambda k=k: nc.tensor.matmul(pt[:], lhsT=wt[:, k, off:off + 128], rhs=xnT[:, k, tb * 512:(tb + 1) * 512],
                                                                        start=(k == 0), stop=(k == KC - 1)),
                                       reads=[wb] + B_xn[tb * 4:tb * 4 + 4], writes=[pb], signal=(k == KC - 1))
                                rows = slice(ci * 128, (ci + 1) * 128)
                                cols = slice(tb * 512, (tb + 1) * 512)
                                if kind == "q":
                                    st, sbf, ss = stbr.next()
                                    op(ACT, lambda: nc.scalar.activation(out=st[:], in_=pt[:], func=AF.Copy, scale=HD ** -0.5), reads=[pb], writes=[sbf])
                                    dma(SP, ss, qT_d[rows, cols], st[:], reads=[sbf], writes=[dr("qT", ci, tb)])
                                elif kind == "k":
                                    st, sbf, ss = stbr.next()
                                    op(DVE, lambda: nc.vector.tensor_copy(st[:], pt[:]), reads=[pb], writes=[sbf])
                                    dma(SP, ss, kT_d[rows, cols], st[:], reads=[sbf], writes=[dr("kT", ci, tb)])
                                elif kind in ("ga", "gb"):
                                    st, sbf, ss = stbr.next()
                                    op(ACT, lambda: nc.scalar.activation(out=st[:], in_=pt[:], func=AF.Silu), reads=[pb], writes=[sbf])
                                    dst = gaT_d if kind == "ga" else gbT_d
                                    dma(SP, ss, dst[rows, cols], st[:], reads=[sbf], writes=[dr(kind, ci, tb)])
                                elif kind == "glua":
                                    op(DVE, lambda: nc.vector.tensor_copy(glu[:, cols], pt[:]), reads=[pb], writes=[Bglu[tb]])
                                elif kind == "glub":
                                    op(ACT, lambda: nc.scalar.activation(out=sig[:], in_=pt[:], func=AF.Sigmoid), reads=[pb], writes=[Bsig])
                                    op(DVE, lambda: nc.vector.tensor_tensor(out=ufp[:], in0=glu[:, cols], in1=sig[:], op=ALU.mult),
                                       reads=[Bglu[tb], Bsig], writes=[Bufp])
                                    st, sbf, ss = stbr.next()
                                    op(ACT, lambda: nc.scalar.copy(out=st[:], in_=ufp[:]), reads=[Bufp], writes=[sbf])
                                    dma(SP, ss, uT_d[rows, cols], st[:], reads=[sbf], writes=[dr("uT", ci, tb)])
                                    if tb == 3:
                                        p2, p2b = psC.next()
                                        op(PE, lambda: nc.tensor.transpose(p2[0:CK - 1, 0:128], ufp[:, 512 - (CK - 1):512], ident_f[:]),
                                           reads=[Bufp, B_const], writes=[p2b])
                                        op(DVE, lambda: nc.vector.tensor_copy(cst[0:CK - 1, rows], p2[0:CK - 1, 0:128]), reads=[p2b], writes=[Bcst])

                        def fm_sample(wt, wb, off, col):
                            pt, pb = psB.next()
                            for k in range(KC):
                                op(PE, lambda k=k: nc.tensor.matmul(pt[:, 0:1], lhsT=wt[:, k, off:off + 128], rhs=xs_b[:, k:k + 1],
                                                                    start=(k == 0), stop=(k == KC - 1)),
                                   reads=[wb, B_xs], writes=[pb], signal=(k == KC - 1))
                            op(DVE, lambda: nc.vector.tensor_copy(us_f[:, col:col + 1], pt[:, 0:1]), reads=[pb], writes=[B_us])

                        def tm_piece(wt, wb, c0, dst_out, want_v):
                            for t in range(NT):
                                pt, pb = psA.next()
                                for k in range(KC):
                                    op(PE, lambda k=k: nc.tensor.matmul(pt[:, 0:PW], lhsT=xnT[:, k, t * 128:(t + 1) * 128], rhs=wt[:, k, :],
                                                                        start=(k == 0), stop=(k == KC - 1)),
                                       reads=[wb, B_xn[t]], writes=[pb], signal=(k == KC - 1))
                                _ktm = int(_os.environ.get("KTM", "9"))
                                if _ktm < 2:
                                    continue
                                st, sbf, ss = stfr.next()
                                op(ACT, lambda: nc.scalar.copy(out=st[:, 0:PW], in_=pt[:, 0:PW]), reads=[pb], writes=[sbf])
                                if _ktm < 3:
                                    continue
                                _dd = {"1": ybuf[t * 128:(t + 1) * 128, c0:c0 + PW], "2": y_p[t * 128:(t + 1) * 128, c0:c0 + PW], "3": nv_p[l, t * 128:(t + 1) * 128, c0:c0 + PW]}.get(_os.environ.get("KV1", ""), dst_out[l, t * 128:(t + 1) * 128, c0:c0 + PW])
                                dma(SP, ss, _dd, st[:, 0:PW], reads=[sbf], writes=[dr("nkv", id(dst_out), l, t, c0)])
                                if want_v:
                                    st2, sbf2, ss2 = stbr.next()
                                    op(DVE, lambda: nc.vector.tensor_copy(st2[:, 0:PW], st[:, 0:PW]), reads=[sbf], writes=[sbf2])
                                    dma(SP, ss2, v_d[t * 128:(t + 1) * 128, c0:c0 + PW], st2[:, 0:PW], reads=[sbf2], writes=[dr("v", t, c0 // 128), dr("v", t, c0 // 128 + 1)])

                        wsrc = w_in[l]
                        sched = []
                        for i in range(8):
                            sched.append(("k", 2048 + i * PW))
                        for i in range(8):
                            sched.append(("v", 4096 + i * PW))
                        for i in range(8):
                            sched.append(("q", i * PW))
                        for i in range(8):
                            sched.append(("ga", 6144 + i * PW))
                        for i in range(8):
                            sched.append(("glu", i))
                        for i in range(8):
                            sched.append(("gb", 12288 + i * PW))

                        sched2 = []
                        for kind, a in sched:
                            if kind == "glu":
                                sched2.append(("glu", 2 * a))
                                sched2.append(("glu", 2 * a + 1))
                            else:
                                sched2.append((kind, a))

                        def issue2(item):
                            kind, a = item
                            if kind == "glu":
                                return load_piece(wsrc, [(8192 + a * 128, 128, 0), (10240 + a * 128, 128, 128)])
                            return load_piece(wsrc, [(a, PW, 0)])

                        _np = int(_os.environ.get("KPIECES", "0"))
                        if _np:
                            sched2 = sched2[:_np]
                        _sk = int(_os.environ.get("KSKIP", "0"))
                        if _sk:
                            sched2 = sched2[_sk:]
                        if _os.environ.get("KNOSAMPLE", "") == "1":
                            fm_sample = lambda *a, **k: None
                        if _os.environ.get("KNOTM", "") == "1":
                            tm_piece = lambda *a, **k: None
                        if _os.environ.get("KNOFM", "") == "1":
                            fm_chunk = lambda *a, **k: None
                        pend = [issue2(sched2[0])]
                        for i, item in enumerate(sched2):
                            if i + 1 < len(sched2):
                                pend.append(issue2(sched2[i + 1]))
                            wt, wb = pend.pop(0)
                            kind, a = item
                            if kind == "k":
                                tm_piece(wt, wb, a - 2048, nk_p, False)
                                for j in range(2):
                                    ci = (a - 2048) // 128 + j
                                    fm_chunk(wt, wb, j * 128, "k", ci)
                                    fm_sample(wt, wb, j * 128, 16 + ci)
                            elif kind == "v":
                                tm_piece(wt, wb, a - 4096, nv_p, True)
                                for j in range(2):
                                    ci = (a - 4096) // 128 + j
                                    fm_sample(wt, wb, j * 128, 32 + ci)
                            elif kind == "q":
                                for j in range(2):
                                    ci = a // 128 + j
                                    fm_chunk(wt, wb, j * 128, "q", ci)
                                    fm_sample(wt, wb, j * 128, ci)
                            elif kind == "ga":
                                for j in range(2):
                                    ci = (a - 6144) // 128 + j
                                    fm_chunk(wt, wb, j * 128, "ga", ci)
                                    fm_sample(wt, wb, j * 128, 48 + ci)
                            elif kind == "glu":
                                fm_chunk(wt, wb, 0, "glua", a)
                                fm_sample(wt, wb, 0, 64 + a)
                                fm_chunk(wt, wb, 128, "glub", a)
                                fm_sample(wt, wb, 128, 80 + a)
                            elif kind == "gb":
                                for j in range(2):
                                    ci = (a - 12288) // 128 + j
                                    fm_chunk(wt, wb, j * 128, "gb", ci)
                                    fm_sample(wt, wb, j * 128, 96 + ci)

                        op(ACT, lambda: nc.scalar.activation(out=small[:, 0:16], in_=us_f[:, 80:96], func=AF.Sigmoid), reads=[B_us], writes=[B_small])
                        op(DVE, lambda: nc.vector.tensor_tensor(out=us_f[:, 64:80], in0=us_f[:, 64:80], in1=small[:, 0:16], op=ALU.mult),
                           reads=[B_us, B_small], writes=[B_us])
                        op(DVE, lambda: nc.vector.tensor_scalar(out=us_f[:, 0:16], in0=us_f[:, 0:16], scalar1=HD ** -0.5, scalar2=None, op0=ALU.mult),
                           reads=[B_us], writes=[B_us])
                        dma(SP, s_misc, nc_p[l], cst[0:CK - 1, :], reads=[Bcst], writes=[dr("ncp", l)])
                        usr = sb(ph1, "usr", [112, 128], F32)
                        Busr = Buf()
                        ptr_, pbr_ = psA.next()
                        op(PE, lambda: nc.tensor.transpose(ptr_[0:112, 0:128], us_f[:, 0:112], ident_f[:]), reads=[B_us, B_const], writes=[pbr_])
                        op(DVE, lambda: nc.vector.tensor_copy(usr[:], ptr_[0:112, 0:128]), reads=[pbr_], writes=[Busr])
                        dma(SP, s_misc, nc_s[l, CK - 2, :].rearrange("(c p) -> c p", p=128), usr[64:80, :], reads=[Busr], writes=[dr("ncs_row", l)])
                        dma(SP, s_misc, nk_s[l, LW - 1, :].rearrange("(c p) -> c p", p=128), usr[16:32, :], reads=[Busr], writes=[dr("nks_row", l, 0)])
                        dma(SP, s_misc, nv_s[l, LW - 1, :].rearrange("(c p) -> c p", p=128), usr[32:48, :], reads=[Busr], writes=[dr("nks_row", l, 1)])
                        barrier()
                        ck_auto()

                with ExitStack() as ph:
                    NBUF = 2
                    hin = []
                    for i in range(NBUF):
                        hin.append(dict(
                            q=sb(ph, f"aq{i}", [128, S], BF16), k=sb(ph, f"ak{i}", [128, S], BF16), ga=sb(ph, f"ag{i}", [128, S], BF16),
                            v1=sb(ph, f"av1{i}", [128, 16, 128], BF16), v4=sb(ph, f"av4{i}", [128, 16, 128], BF16),
                            v16=sb(ph, f"av16{i}", [128, 16, 128], BF16), tb=sb(ph, f"atb{i}", [128, 3, 256], F32),
                            b=Buf(), s=s_ld[i]))
                    ex = [(sb(ph, f"ex{i}", [128, 512], F32), Buf()) for i in range(3)]
                    exr = Ring(ex)
                    pp = [(sb(ph, f"pp{i}", [128, 512], BF16), Buf()) for i in range(4)]
                    ppr = Ring(pp)
                    rd = sb(ph, "rd", [128, 512], F32)
                    of = sb(ph, "of", [128, 512], F32)
                    Brd, Bof = Buf(), Buf()
                    yst = [(sb(ph, f"yst{i}", [128, 512], BF16), Buf(), s_st[i]) for i in range(2)]
                    ystr = Ring(yst)
                    kc = sb(ph, "kc", [128, ATT], F32)
                    vc = sb(ph, "vc", [128, ATT], F32)
                    kcT = sb(ph, "kcT", [128, H, 128], F32)
                    Bkc, Bvc, BkcT = Buf(), Buf(), Buf()
                    sm = sb(ph, "sm", [128, 128], F32)
                    Bsm = Buf()

                    def load_head(h, slot):
                        d = hin[slot]
                        rows = slice(h * 128, (h + 1) * 128)
                        b, s = d["b"], d["s"]
                        rd_ = [dr("qT", h, tb) for tb in range(4)] + [dr("kT", h, tb) for tb in range(4)] + [dr("ga", h, tb) for tb in range(4)] \
                            + [dr("v", t, h) for t in range(NT)] + [dr("R", h)]
                        _deps(SP, rd_, [b])
                        SP.wait(s, s.issued)
                        insts = []
                        insts.append(nc.sync.dma_start(out=d["q"][:], in_=qT_d[rows, :]))
                        insts.append(nc.sync.dma_start(out=d["k"][:], in_=kT_d[rows, :]))
                        insts.append(nc.sync.dma_start(out=d["ga"][:], in_=gaT_d[rows, :]))
                        vh = v_d[:, rows]
                        insts.append(nc.sync.dma_start(out=d["v1"][:], in_=vh.rearrange("(n j) e -> j n e", j=128)))
                        for m_ in range(4):
                            insts.append(nc.sync.dma_start(out=d["v4"][:, m_ * 4:(m_ + 1) * 4, :],
                                                           in_=vh[m_ * 512:(m_ + 1) * 512, :].rearrange("(j r) e -> j r e", r=4)))
                        insts.append(nc.sync.dma_start(out=d["v16"][:], in_=vh.rearrange("(j r) e -> j r e", r=16)))
                        for p in range(3):
                            src = bass.AP(tensor=R_d.tensor, offset=R_d[h].offset + p * TL + 127, ap=[[3 * TL - 1, 128], [1, 256]])
                            insts.append(nc.sync.dma_start(out=d["tb"][:, p, :], in_=src))
                        for ins in insts:
                            ins.then_inc(s.h, 16)
                            s.issued += 16
                        b.writers = {s: s.issued}
                        b.readers = {}
                        for x in rd_:
                            x.readers[s] = s.issued

                    load_head(0, 0)
                    for h in range(H):
                        if h + 1 < H:
                            load_head(h + 1, (h + 1) % NBUF)
                        d = hin[h % NBUF]
                        hb = d["b"]
                        q, k, ga, tbm = d["q"], d["k"], d["ga"], d["tb"]
                        for c in range(4):
                            po, pob = psB.next()
                            pd, pdb = psC.next()
                            first = [True]

                            def acc(out_o, out_d, vt, prhs, pbuf):
                                st_ = first[0]
                                first[0] = False
                                op(PE, lambda: nc.tensor.matmul(out_o, lhsT=vt, rhs=prhs, start=st_, stop=False, skip_group_check=True),
                                   reads=[hb, pbuf], writes=[pob], signal=False)
                                op(PE, lambda: nc.tensor.matmul(out_d, lhsT=ones_b[:], rhs=prhs, start=st_, stop=False, skip_group_check=True),
                                   reads=[B_const, pbuf], writes=[pdb], signal=True)

                            for half in range(2):
                                ps_, psb_ = psA.next()
                                blocks = []
                                for jj in range(2):
                                    n = c * 4 + half * 2 + jj
                                    qs = q[:, n * 128:(n + 1) * 128]
                                    op(PE, lambda: nc.tensor.matmul(ps_[:, jj * 256:jj * 256 + 128], lhsT=k[:, n * 128:(n + 1) * 128], rhs=qs,
                                                                    start=True, stop=True, skip_group_check=True), reads=[hb], writes=[psb_], signal=(n == 0 and jj == 1))
                                    if n > 0:
                                        op(PE, lambda: nc.tensor.matmul(ps_[:, jj * 256 + 128:jj * 256 + 256], lhsT=k[:, (n - 1) * 128:n * 128], rhs=qs,
                                                                        start=True, stop=True, skip_group_check=True), reads=[hb], writes=[psb_], signal=(jj == 1))
                                    blocks.append(n)
                                e_, eb_ = exr.next()
                                p_, pb_ = ppr.next()
                                if blocks[0] == 0:
                                    op(ACT, lambda: nc.scalar.activation(out=e_[:, 0:128], in_=ps_[:, 0:128], func=AF.Exp), reads=[psb_], writes=[eb_])
                                    op(ACT, lambda: nc.scalar.activation(out=e_[:, 256:512], in_=ps_[:, 256:512], func=AF.Exp), reads=[psb_], writes=[eb_])
                                    op(DVE, lambda: nc.vector.tensor_tensor(out=p_[:, 0:128], in0=e_[:, 0:128], in1=tbm[:, 0, 0:128], op=ALU.mult),
                                       reads=[eb_, hb], writes=[pb_])
                                    op(DVE, lambda: nc.vector.tensor_tensor(out=p_[:, 256:512], in0=e_[:, 256:512], in1=tbm[:, 0, :], op=ALU.mult),
                                       reads=[eb_, hb], writes=[pb_])
                                else:
                                    op(ACT, lambda: nc.scalar.activation(out=e_[:], in_=ps_[:], func=AF.Exp), reads=[psb_], writes=[eb_])
                                    op(DVE, lambda: nc.vector.tensor_tensor(out=p_[:].rearrange("p (a x) -> p a x", a=2),
                                                                            in0=e_[:].rearrange("p (a x) -> p a x", a=2),
                                                                            in1=rap(tbm[:, 0, :], 0, [[0, 2], [1, 256]]), op=ALU.mult),
                                       reads=[eb_, hb], writes=[pb_])
                                for jj, n in enumerate(blocks):
                                    oc = (n - c * 4) * 128
                                    acc(po[:, oc:oc + 128], pd[:, oc:oc + 128], d["v1"][:, n, :], p_[:, jj * 256:jj * 256 + 128], pb_)
                                    if n > 0:
                                        acc(po[:, oc:oc + 128], pd[:, oc:oc + 128], d["v1"][:, n - 1, :], p_[:, jj * 256 + 128:jj * 256 + 256], pb_)
                            for half in range(2):
                                ps_, psb_ = psA.next()
                                for jj in range(2):
                                    r = half * 2 + jj
                                    qs = rap(q[:], c * 512 + r, [[4, 128]])
                                    op(PE, lambda: nc.tensor.matmul(ps_[:, jj * 256:jj * 256 + 128], lhsT=rap(k[:], c * 512 + r, [[4, 128]]), rhs=qs,
                                                                    start=True, stop=True, skip_group_check=True), reads=[hb], writes=[psb_], signal=(c == 0 and jj == 1))
                                    if c > 0:
                                        op(PE, lambda: nc.tensor.matmul(ps_[:, jj * 256 + 128:jj * 256 + 256], lhsT=rap(k[:], (c - 1) * 512 + r, [[4, 128]]), rhs=qs,
                                                                        start=True, stop=True, skip_group_check=True), reads=[hb], writes=[psb_], signal=(jj == 1))
                                e_, eb_ = exr.next()
                                p_, pb_ = ppr.next()
                                if c == 0:
                                    for jj in range(2):
                                        op(ACT, lambda jj=jj: nc.scalar.activation(out=e_[:, jj * 256:jj * 256 + 128], in_=ps_[:, jj * 256:jj * 256 + 128], func=AF.Exp),
                                           reads=[psb_], writes=[eb_])
                                        op(DVE, lambda jj=jj: nc.vector.tensor_tensor(out=p_[:, jj * 256:jj * 256 + 128], in0=e_[:, jj * 256:jj * 256 + 128],
                                                                                      in1=tbm[:, 1, 0:128], op=ALU.mult), reads=[eb_, hb], writes=[pb_])
                                else:
                                    op(ACT, lambda: nc.scalar.activation(out=e_[:], in_=ps_[:], func=AF.Exp), reads=[psb_], writes=[eb_])
                                    op(DVE, lambda: nc.vector.tensor_tensor(out=p_[:].rearrange("p (a x) -> p a x", a=2),
                                                                            in0=e_[:].rearrange("p (a x) -> p a x", a=2),
                                                                            in1=rap(tbm[:, 1, :], 0, [[0, 2], [1, 256]]), op=ALU.mult),
                                       reads=[eb_, hb], writes=[pb_])
                                for jj in range(2):
                                    r = half * 2 + jj
                                    oo = rap(po[:], r, [[4, 128]])
                                    od = rap(pd[:], r, [[4, 128]])
                                    acc(oo, od, d["v4"][:, c * 4 + r, :], p_[:, jj * 256:jj * 256 + 128], pb_)
                                    if c > 0:
                                        acc(oo, od, d["v4"][:, (c - 1) * 4 + r, :], p_[:, jj * 256 + 128:jj * 256 + 256], pb_)
                            ps_, psb_ = psA.next()
                            for r in range(16):
                                op(PE, lambda r=r: nc.tensor.matmul(ps_[:, r * 32:(r + 1) * 32], lhsT=rap(k[:], r, [[16, 128]]),
                                                                    rhs=rap(q[:], c * 512 + r, [[16, 32]]), start=True, stop=True, skip_group_check=True),
                                   reads=[hb], writes=[psb_], signal=(r == 15))
                            e_, eb_ = exr.next()
                            p_, pb_ = ppr.next()
                            op(ACT, lambda: nc.scalar.activation(out=e_[:], in_=ps_[:], func=AF.Exp), reads=[psb_], writes=[eb_])
                            op(DVE, lambda: nc.vector.tensor_tensor(out=p_[:].rearrange("p (a x) -> p a x", a=16),
                                                                    in0=e_[:].rearrange("p (a x) -> p a x", a=16),
                                                                    in1=rap(tbm[:, 2, :], c * 32, [[0, 16], [1, 32]]), op=ALU.mult),
                               reads=[eb_, hb], writes=[pb_])
                            for r in range(16):
                                acc(rap(po[:], r, [[16, 32]]), rap(pd[:], r, [[16, 32]]), d["v16"][:, r, :], p_[:, r * 32:(r + 1) * 32], pb_)
                            op(DVE, lambda: nc.vector.reciprocal(rd[:], pd[:]), reads=[pdb], writes=[Brd])
                            op(DVE, lambda: nc.vector.tensor_tensor(out=of[:], in0=po[:], in1=rd[:], op=ALU.mult), reads=[pob, Brd], writes=[Bof])
                            st, sbf, ss = ystr.next()
                            op(DVE, lambda: nc.vector.tensor_tensor(out=st[:], in0=of[:], in1=ga[:, c * 512:(c + 1) * 512], op=ALU.mult),
                               reads=[Bof, hb], writes=[sbf])
                            dma(SP, ss, ycT_d[h * 128:(h + 1) * 128, c * 512:(c + 1) * 512], st[:], reads=[sbf], writes=[dr("ycT", h, c)])

                    qs_ = us_f[:, 0:16]
                    ks_ = us_f[:, 16:32]
                    vs_ = us_f[:, 32:48]
                    pso, psob = psB.next()
                    psd, psdb = psC.next()
                    firsto = True
                    for p, dil in enumerate(DILS):
                        r0 = LW - 128 * dil
                        ksrc = bass.AP(tensor=ck_in.tensor, offset=ck_in[l].offset + r0 * ATT, ap=[[dil * ATT, 128], [1, ATT]])
                        vsrc = bass.AP(tensor=cv_in.tensor, offset=cv_in[l].offset + r0 * ATT, ap=[[dil * ATT, 128], [1, ATT]])
                        dma(SP, s_ld[2], kc[:], ksrc, writes=[Bkc])
                        dma(SP, s_ld[3], vc[:], vsrc, writes=[Bvc])
                        for h4 in range(4):
                            pt, pb = psA.next()
                            for j in range(4):
                                hh = h4 * 4 + j
                                op(PE, lambda hh=hh, j=j: nc.tensor.transpose(pt[:, j * 128:(j + 1) * 128], kc[:, hh * 128:(hh + 1) * 128], ident_f[:]),
                                   reads=[Bkc, B_const], writes=[pb])
                            op(ACT, lambda h4=h4: nc.scalar.copy(out=kcT[:, h4 * 4:h4 * 4 + 4, :].rearrange("p a x -> p (a x)"), in_=pt[:]),
                               reads=[pb], writes=[BkcT])
                        pt, pb = psA.next()
                        for hh in range(H):
                            op(PE, lambda hh=hh: nc.tensor.matmul(pt[:, hh:hh + 1], lhsT=kcT[:, hh, :], rhs=qs_[:, hh:hh + 1], start=True, stop=True,
                                                                  skip_group_check=True), reads=[BkcT, B_us], writes=[pb])
                        op(ACT, lambda: nc.scalar.activation(out=sm[:, 0:16], in_=pt[:, 0:16], func=AF.Exp), reads=[pb], writes=[Bsm])
                        op(DVE, lambda p=p: nc.vector.tensor_tensor(out=sm[:, 16:32], in0=sm[:, 0:16], in1=EBs[:, p, :], op=ALU.mult),
                           reads=[Bsm, B_const], writes=[Bsm])
                        for hh in range(H):
                            st_ = firsto
                            firsto = False
                            op(PE, lambda hh=hh, st_=st_: nc.tensor.matmul(pso[:, hh:hh + 1], lhsT=vc[:, hh * 128:(hh + 1) * 128], rhs=sm[:, 16 + hh:17 + hh],
                                                                           start=st_, stop=False, skip_group_check=True), reads=[Bvc, Bsm], writes=[psob])
                        op(PE, lambda p=p: nc.tensor.matmul(psd[:, 0:16], lhsT=ones_f[:], rhs=sm[:, 16:32], start=(p == 0), stop=False, skip_group_check=True),
                           reads=[Bsm, B_const], writes=[psdb])
                    op(DVE, lambda: nc.vector.tensor_tensor(out=sm[:, 32:48], in0=qs_, in1=ks_, op=ALU.mult), reads=[B_us], writes=[Bsm])
                    pt, pb = psA.next()
                    op(PE, lambda: nc.tensor.matmul(pt[:, 0:16], lhsT=ones_f[:], rhs=sm[:, 32:48], start=True, stop=True), reads=[Bsm, B_const], writes=[pb])
                    op(ACT, lambda: nc.scalar.activation(out=sm[:, 48:64], in_=pt[:, 0:16], func=AF.Exp), reads=[pb], writes=[Bsm])
                    op(DVE, lambda: nc.vector.scalar_tensor_tensor(out=sm[:, 48:64], in0=sm[:, 48:64], scalar=3.0, in1=E0bc[:], op0=ALU.mult, op1=ALU.mult),
                       reads=[Bsm, B_const], writes=[Bsm])
                    op(DVE, lambda: nc.vector.tensor_tensor(out=sm[:, 64:80], in0=sm[:, 48:64], in1=vs_, op=ALU.mult), reads=[Bsm, B_us], writes=[Bsm])
                    op(DVE, lambda: nc.vector.tensor_tensor(out=sm[:, 64:80], in0=sm[:, 64:80], in1=pso[:, 0:16], op=ALU.add), reads=[Bsm, psob], writes=[Bsm])
                    op(DVE, lambda: nc.vector.tensor_tensor(out=sm[:, 80:96], in0=sm[:, 48:64], in1=psd[:, 0:16], op=ALU.add), reads=[Bsm, psdb], writes=[Bsm])
                    op(DVE, lambda: nc.vector.reciprocal(sm[:, 96:112], sm[:, 80:96]), reads=[Bsm], writes=[Bsm])
                    op(DVE, lambda: nc.vector.tensor_tensor(out=sm[:, 64:80], in0=sm[:, 64:80], in1=sm[:, 96:112], op=ALU.mult), reads=[Bsm], writes=[Bsm])
                    op(ACT, lambda: nc.scalar.activation(out=sm[:, 112:128], in_=us_f[:, 48:64], func=AF.Silu), reads=[B_us], writes=[Bsm])
                    op(DVE, lambda: nc.vector.tensor_tensor(out=ycs_b[:, 0:16], in0=sm[:, 64:80], in1=sm[:, 112:128], op=ALU.mult), reads=[Bsm], writes=[B_ycs])
                    barrier()
                    ck_auto()

                with ExitStack() as ph:
                    HALO = CK - 1
                    ut = [(sb(ph, f"ut{i}", [128, 16, HALO + 512], BF16), Buf(), s_ld[i]) for i in range(2)]
                    gbt = [(sb(ph, f"gbt{i}", [128, 16, 512], BF16), Buf(), s_ld[2 + i]) for i in range(2)]
                    dg = [(sb(ph, f"dg{i}", [128, CK, 128], BF16), Buf()) for i in range(2)]
                    dgr = Ring(dg)
                    identb3 = rap(ident_b[:], 0, [[0, CK], [1, 128]])
                    cwTb = sb(ph, "cwTb", [128, 16, CK], BF16)
                    BcwTb = Buf()
                    op(DVE, lambda: nc.vector.tensor_copy(cwTb[:], cwT[:]), reads=[B_cwT], writes=[BcwTb])
                    yc = sb(ph, "yc", [128, 16, 512], F32)
                    Byc = [Buf() for _ in range(16)]
                    ybq = [(sb(ph, f"ybq{i}", [128, 512], BF16), Buf()) for i in range(2)]
                    ysq = [(sb(ph, f"ysq{i}", [128, 512], BF16), Buf()) for i in range(2)]
                    ybr, ysr = Ring(ybq), Ring(ysq)
                    mean = sb(ph, "mean", [128, 512], F32)
                    rstd = sb(ph, "rstd", [128, 512], F32)
                    tmpc = sb(ph, "tmpc", [128, 512], F32)
                    Bmean, Brstd, Btmp = Buf(), Buf(), Buf()
                    tn = [(sb(ph, f"tn{i}", [128, 512], F32), Buf()) for i in range(2)]
                    tnr = Ring(tn)
                    sl = [(sb(ph, f"sl{i}", [128, 512], F32), Buf()) for i in range(2)]
                    slr = Ring(sl)
                    yst = [(sb(ph, f"cyst{i}", [128, 512], BF16), Buf(), s_st[i]) for i in range(2)]
                    ystr = Ring(yst)

                    def load_tb(tb):
                        u_, ub_, us_ = ut[tb % 2]
                        g_, gb_, gs_ = gbt[tb % 2]
                        uv = uT_d.rearrange("(c p) t -> p c t", p=128)
                        gv = gbT_d.rearrange("(c p) t -> p c t", p=128)
                        rdu = [dr("uT", ci, x) for ci in range(16) for x in range(4)]
                        if tb == 0:
                            op(DVE, lambda: nc.vector.memset(u_[:, :, 0:HALO], 0.0), writes=[ub_])
                            dma(SP, us_, u_[:, :, HALO:HALO + 512], uv[:, :, 0:512], reads=rdu, writes=[ub_])
                        else:
                            dma(SP, us_, u_[:], uv[:, :, tb * 512 - HALO:(tb + 1) * 512], reads=rdu, writes=[ub_])
                        dma(SP, gs_, g_[:], gv[:, :, tb * 512:(tb + 1) * 512], reads=[dr("gb", ci, x) for ci in range(16) for x in range(4)], writes=[gb_])

                    load_tb(0)
                    for tb in range(4):
                        if tb + 1 < 4:
                            load_tb(tb + 1)
                        u_, ub_, _ = ut[tb % 2]
                        g_, gb_, _ = gbt[tb % 2]
                        psm, psmb = psB.next()
                        pss, pssb = psC.next()
                        for c in range(16):
                            dgt, dgb = dgr.next()
                            op(PL, lambda c=c: nc.gpsimd.tensor_tensor(out=dgt[:], in0=identb3, in1=rap(cwTb[:, c, :], 0, [[1, CK], [0, 128]]), op=ALU.mult),
                               reads=[BcwTb, B_const], writes=[dgb])
                            pt, pb = psA.next()
                            for j in range(CK):
                                op(PE, lambda j=j: nc.tensor.matmul(pt[:], lhsT=dgt[:, j, :], rhs=u_[:, c, j:j + 512], start=(j == 0), stop=(j == CK - 1)),
                                   reads=[dgb, ub_], writes=[pb], signal=(j == CK - 1))
                            op(ACT, lambda c=c: nc.scalar.activation(out=yc[:, c, :], in_=pt[:], func=AF.Identity, bias=cbT[:, c:c + 1]),
                               reads=[pb, B_vec], writes=[Byc[c]])
                            yb_, ybb_ = ybr.next()
                            ys_, ysb_ = ysr.next()
                            op(DVE, lambda c=c: nc.vector.tensor_copy(yb_[:], yc[:, c, :]), reads=[Byc[c]], writes=[ybb_])
                            op(ACT, lambda c=c: nc.scalar.activation(out=ys_[:], in_=yc[:, c, :], func=AF.Square), reads=[Byc[c]], writes=[ysb_])
                            op(PE, lambda c=c: nc.tensor.matmul(psm[:], lhsT=ones_b[:], rhs=yb_[:], start=(c == 0), stop=(c == 15)),
                               reads=[ybb_, B_const], writes=[psmb], signal=True)
                            op(PE, lambda c=c: nc.tensor.matmul(pss[:], lhsT=ones_b[:], rhs=ys_[:], start=(c == 0), stop=(c == 15)),
                               reads=[ysb_, B_const], writes=[pssb], signal=True)
                        op(DVE, lambda: nc.vector.tensor_scalar(out=mean[:], in0=psm[:], scalar1=1.0 / CC, scalar2=None, op0=ALU.mult), reads=[psmb], writes=[Bmean])
                        op(DVE, lambda: nc.vector.tensor_tensor(out=tmpc[:], in0=mean[:], in1=mean[:], op=ALU.mult), reads=[Bmean], writes=[Btmp])
                        op(DVE, lambda: nc.vector.scalar_tensor_tensor(out=tmpc[:], in0=pss[:], scalar=1.0 / CC, in1=tmpc[:], op0=ALU.mult, op1=ALU.subtract),
                           reads=[pssb, Btmp], writes=[Btmp])
                        op(DVE, lambda: nc.vector.tensor_scalar(out=tmpc[:], in0=tmpc[:], scalar1=EPS, scalar2=None, op0=ALU.add), reads=[Btmp], writes=[Btmp])
                        op(ACT, lambda: nc.scalar.activation(out=tmpc[:], in_=tmpc[:], func=AF.Sqrt), reads=[Btmp], writes=[Btmp])
                        op(DVE, lambda: nc.vector.reciprocal(rstd[:], tmpc[:]), reads=[Btmp], writes=[Brstd])
                        for c in range(16):
                            t_, tb_ = tnr.next()
                            op(DVE, lambda c=c: nc.vector.tensor_tensor(out=t_[:], in0=yc[:, c, :], in1=mean[:], op=ALU.subtract), reads=[Byc[c], Bmean], writes=[tb_])
                            op(DVE, lambda: nc.vector.tensor_tensor(out=t_[:], in0=t_[:], in1=rstd[:], op=ALU.mult), reads=[tb_, Brstd], writes=[tb_])
                            s_, sb_ = slr.next()
                            op(ACT, lambda c=c: nc.scalar.activation(out=s_[:], in_=t_[:], func=AF.Silu, scale=lngT[:, c:c + 1], bias=lnbT[:, c:c + 1]),
                               reads=[tb_, B_vec], writes=[sb_])
                            st, sbf, ss = ystr.next()
                            op(DVE, lambda c=c: nc.vector.tensor_tensor(out=st[:], in0=s_[:], in1=g_[:, c, :], op=ALU.mult), reads=[sb_, gb_], writes=[sbf])
                            dma(SP, ss, ycT_d[(16 + c) * 128:(17 + c) * 128, tb * 512:(tb + 1) * 512], st[:], reads=[sbf], writes=[dr("ycT", 16 + c, tb)])

                    scr = sb(ph, "scr", [CK - 1, CC], F32)
                    ucs = sb(ph, "ucs", [128, 16, CK], F32)
                    Bscr, Bucs = Buf(), Buf()
                    dma(SP, s_misc, scr[:], sc_in[l], writes=[Bscr])
                    for c4 in range(4):
                        pt, pb = psA.next()
                        for j in range(4):
                            c = c4 * 4 + j
                            op(PE, lambda c=c, j=j: nc.tensor.transpose(pt[:, j * 32:j * 32 + CK - 1], scr[:, c * 128:(c + 1) * 128], ident_f[0:CK - 1, 0:CK - 1]),
                               reads=[Bscr, B_const], writes=[pb])
                        op(DVE, lambda c4=c4: nc.vector.tensor_copy(ucs[:, c4 * 4:c4 * 4 + 4, 0:CK - 1], pt[:, 0:128].rearrange("p (j t) -> p j t", t=32)[:, :, 0:CK - 1]),
                           reads=[pb], writes=[Bucs])
                    op(DVE, lambda: nc.vector.tensor_copy(ucs[:, :, CK - 1:CK], us_f[:, 64:80].rearrange("p (c o) -> p c o", o=1)), reads=[B_us], writes=[Bucs])
                    op(DVE, lambda: nc.vector.tensor_tensor(out=ucs[:], in0=ucs[:], in1=cwT[:], op=ALU.mult), reads=[Bucs, B_cwT], writes=[Bucs])
                    op(DVE, lambda: nc.vector.reduce_sum(out=small[:, 0:16], in_=ucs[:], axis=AX.X), reads=[Bucs], writes=[B_small])
                    op(DVE, lambda: nc.vector.tensor_tensor(out=small[:, 0:16], in0=small[:, 0:16], in1=cbT, op=ALU.add), reads=[B_small, B_vec], writes=[B_small])
                    op(DVE, lambda: nc.vector.reduce_sum(out=small[:, 16:17], in_=small[:, 0:16], axis=AX.X), reads=[B_small], writes=[B_small])
                    op(DVE, lambda: nc.vector.tensor_tensor(out=small[:, 32:48], in0=small[:, 0:16], in1=small[:, 0:16], op=ALU.mult), reads=[B_small], writes=[B_small])
                    op(DVE, lambda: nc.vector.reduce_sum(out=small[:, 17:18], in_=small[:, 32:48], axis=AX.X), reads=[B_small], writes=[B_small])
                    pt, pb = psA.next()
                    op(PE, lambda: nc.tensor.matmul(pt[:, 0:2], lhsT=ones_f[:], rhs=small[:, 16:18], start=True, stop=True), reads=[B_small, B_const], writes=[pb])
                    op(DVE, lambda: nc.vector.tensor_scalar(out=small[:, 18:20], in0=pt[:, 0:2], scalar1=1.0 / CC, scalar2=None, op0=ALU.mult), reads=[pb], writes=[B_small])
                    op(DVE, lambda: nc.vector.tensor_tensor(out=small[:, 20:21], in0=small[:, 18:19], in1=small[:, 18:19], op=ALU.mult), reads=[B_small], writes=[B_small])
                    op(DVE, lambda: nc.vector.tensor_tensor(out=small[:, 20:21], in0=small[:, 19:20], in1=small[:, 20:21], op=ALU.subtract), reads=[B_small], writes=[B_small])
                    op(DVE, lambda: nc.vector.tensor_scalar(out=small[:, 20:21], in0=small[:, 20:21], scalar1=EPS, scalar2=None, op0=ALU.add), reads=[B_small], writes=[B_small])
                    op(ACT, lambda: nc.scalar.activation(out=small[:, 21:22], in_=small[:, 20:21], func=AF.Sqrt), reads=[B_small], writes=[B_small])
                    op(DVE, lambda: nc.vector.reciprocal(small[:, 22:23], small[:, 21:22]), reads=[B_small], writes=[B_small])
                    op(DVE, lambda: nc.vector.tensor_scalar(out=small[:, 0:16], in0=small[:, 0:16], scalar1=small[:, 18:19], scalar2=small[:, 22:23],
                                                            op0=ALU.subtract, op1=ALU.mult), reads=[B_small], writes=[B_small])
                    op(DVE, lambda: nc.vector.tensor_tensor(out=small[:, 0:16], in0=small[:, 0:16], in1=lngT, op=ALU.mult), reads=[B_small, B_vec], writes=[B_small])
                    op(DVE, lambda: nc.vector.tensor_tensor(out=small[:, 0:16], in0=small[:, 0:16], in1=lnbT, op=ALU.add), reads=[B_small, B_vec], writes=[B_small])
                    op(ACT, lambda: nc.scalar.activation(out=small[:, 32:48], in_=small[:, 0:16], func=AF.Silu), reads=[B_small], writes=[B_small])
                    op(ACT, lambda: nc.scalar.activation(out=small[:, 48:64], in_=us_f[:, 96:112], func=AF.Silu), reads=[B_us], writes=[B_small])
                    op(DVE, lambda: nc.vector.tensor_tensor(out=ycs_b[:, 16:32], in0=small[:, 32:48], in1=small[:, 48:64], op=ALU.mult), reads=[B_small], writes=[B_ycs])
                    barrier()
                    ck_auto()

                NPC = D // PW
                with ExitStack() as ph:
                    ssq = sb(ph, "ssq", [128, NT, NPC], F32)
                    Bssq = Buf()
                    rsy = sb(ph, "rsy", [128, NT], F32)
                    Brsy = Buf()
                    with ExitStack() as ph4:
                        ycT = sb(ph4, "ycT", [128, KC, S], BF16)
                        Byc_ = [Buf() for _ in range(KC)]
                        ycv = ycT_d.rearrange("(k p) t -> p k t", p=128)
                        for kq in range(8):
                            dma(SP, s_ld[kq % 4], ycT[:, kq * 4:(kq + 1) * 4, :], ycv[:, kq * 4:(kq + 1) * 4, :],
                                reads=[dr("ycT", kk, x) for kk in range(kq * 4, kq * 4 + 4) for x in range(4)], writes=Byc_[kq * 4:(kq + 1) * 4])
                        stf = [(sb(ph4, f"wstf{i}", [128, PW], F32), Buf(), s_st[i]) for i in range(3)]
                        stfr = Ring(stf)
                        junk = sb(ph4, "junk", [128, PW], BF16)
                        Bjunk = Buf()
                        op(DVE, lambda: nc.vector.memset(ssq[:], 0.0), writes=[Bssq])
                        wsrc = w_out[l]
                        pend = [load_piece(wsrc, [(0, PW, 0)])]
                        for pi in range(NPC):
                            if pi + 1 < NPC:
                                pend.append(load_piece(wsrc, [((pi + 1) * PW, PW, 0)]))
                            wt, wb = pend.pop(0)
                            for t in range(NT):
                                pt, pb = psA.next()
                                for k in range(KC):
                                    op(PE, lambda k=k: nc.tensor.matmul(pt[:, 0:PW], lhsT=ycT[:, k, t * 128:(t + 1) * 128], rhs=wt[:, k, :],
                                                                        start=(k == 0), stop=(k == KC - 1)),
                                       reads=[wb, Byc_[k]], writes=[pb], signal=(k == KC - 1))
                                st, sbf, ss = stfr.next()
                                op(ACT, lambda: nc.scalar.copy(out=st[:], in_=pt[:, 0:PW]), reads=[pb], writes=[sbf])
                                op(ACT, lambda t=t, pi=pi: nc.scalar.activation(out=junk[:], in_=st[:], func=AF.Square, accum_out=ssq[:, t, pi:pi + 1]),
                                   reads=[sbf], writes=[Bjunk, Bssq])
                                dma(SP, ss, ybuf[t * 128:(t + 1) * 128, pi * PW:(pi + 1) * PW], st[:], reads=[sbf], writes=[dr("ybuf", t, pi // 4)])
                            pt, pb = psB.next()
                            for j in range(2):
                                for k in range(KC):
                                    op(PE, lambda k=k, j=j: nc.tensor.matmul(pt[:, j:j + 1], lhsT=wt[:, k, j * 128:(j + 1) * 128], rhs=ycs_b[:, k:k + 1],
                                                                             start=(k == 0), stop=(k == KC - 1), skip_group_check=True),
                                       reads=[wb, B_ycs], writes=[pb], signal=(k == KC - 1))
                            op(DVE, lambda pi=pi: nc.vector.tensor_copy(ys_f[:, 2 * pi:2 * pi + 2], pt[:, 0:2]), reads=[pb], writes=[B_ys])
                        op(DVE, lambda: nc.vector.reduce_sum(out=rsy[:], in_=ssq[:], axis=AX.X), reads=[Bssq], writes=[Brsy])
                        op(DVE, lambda: nc.vector.tensor_scalar(out=rsy[:], in0=rsy[:], scalar1=1.0 / D, scalar2=EPS, op0=ALU.mult, op1=ALU.add), reads=[Brsy], writes=[Brsy])
                        op(ACT, lambda: nc.scalar.activation(out=rsy[:], in_=rsy[:], func=AF.Sqrt), reads=[Brsy], writes=[Brsy])
                        op(DVE, lambda: nc.vector.reciprocal(rsy[:], rsy[:]), reads=[Brsy], writes=[Brsy])
                        if _dbg:
                            dma(SP, s_misc, dbg_rsy[:, :], rsy[:], reads=[Brsy])
                        barrier()
                        ck_auto()

                    with ExitStack() as phv:
                        vrow = sb(phv, "vrow2", [128, 128], F32)
                        Bv = Buf()
                        op(DVE, lambda: nc.vector.memset(vrow[:], 0.0), writes=[Bv])
                        dma(SP, s_misc, vrow[0:32, :], g_post[l].rearrange("(k p) -> k p", p=128), writes=[Bv])
                        dma(SP, s_misc, vrow[32:64, :], g_ple[l].rearrange("(k p) -> k p", p=128), writes=[Bv])
                        dma(SP, s_misc, vrow[64:96, :], b_pg[l].rearrange("(k p) -> k p", p=128), writes=[Bv])
                        pt, pb = psA.next()
                        op(PE, lambda: nc.tensor.transpose(pt[:, 0:128], vrow[:], ident_f[:]), reads=[Bv, B_const], writes=[pb])
                        op(DVE, lambda: nc.vector.tensor_copy(vec[:, 4:7, :].rearrange("p a k -> p (a k)"), pt[:, 0:96]), reads=[pb], writes=[B_vec])
                        barrier()
                        ck_auto()
                    gpostT, gpleT, bpgT = vec[:, 4, :], vec[:, 5, :], vec[:, 6, :]

                    hmT = sb(ph, "hmT", [128, KC, S], BF16)
                    B_hm = [Buf() for _ in range(NT)]
                    with ExitStack() as ph5:
                        QW = 1024
                        gq = sb(ph5, "gq", [128, QW], F32)
                        Bgq = Buf()
                        yq = [(sb(ph5, f"yq{i}", [128, QW], F32), Buf(), s_ld[i]) for i in range(2)]
                        hq = [(sb(ph5, f"hq{i}", [128, QW], F32), Buf(), s_ld[2 + i]) for i in range(2)]
                        hb_ = [(sb(ph5, f"hbq{i}", [128, QW], BF16), Buf()) for i in range(2)]
                        yqr, hqr, hbr = Ring(yq), Ring(hq), Ring(hb_)
                        for cq in range(D // QW):
                            cs = slice(cq * QW, (cq + 1) * QW)
                            dma(SP, s_misc, gq[:], bass.AP(tensor=g_post.tensor, offset=g_post[l].offset + cq * QW, ap=[[0, 128], [1, QW]]), writes=[Bgq])
                            for t in range(NT):
                                y_, yb_, ysm = yqr.next()
                                h_, hbf_, hsm = hqr.next()
                                ts = slice(t * 128, (t + 1) * 128)
                                dma(SP, ysm, y_[:], ybuf[ts, cs], reads=[dr("ybuf", t, cq)], writes=[yb_])
                                dma(SP, hsm, h_[:], h_src[ts, cs], reads=[dr(hkey, t)], writes=[hbf_])
                                op(DVE, lambda t=t: nc.vector.scalar_tensor_tensor(out=y_[:], in0=y_[:], scalar=rsy[:, t:t + 1], in1=gq[:], op0=ALU.mult, op1=ALU.mult),
                                   reads=[yb_, Brsy, Bgq], writes=[yb_])
                                op(DVE, lambda: nc.vector.tensor_tensor(out=h_[:], in0=h_[:], in1=y_[:], op=ALU.add), reads=[hbf_, yb_], writes=[hbf_])
                                dma(SP, hsm, hmid[ts, cs], h_[:], reads=[hbf_], writes=[dr("hmid", t, cq)])
                                b16, b16b = hbr.next()
                                op(ACT, lambda: nc.scalar.copy(out=b16[:], in_=h_[:]), reads=[hbf_], writes=[b16b])
                                for k4 in range(QW // 512):
                                    pt, pb = psA.next()
                                    ptb = pt[:].bitcast(BF16)
                                    for j in range(4):
                                        op(PE, lambda k4=k4, j=j: nc.tensor.transpose(ptb[:, j * 128:(j + 1) * 128], b16[:, (k4 * 4 + j) * 128:(k4 * 4 + j + 1) * 128], ident_b[:]),
                                           reads=[b16b, B_const], writes=[pb], signal=(j == 3))
                                    kk = cq * (QW // 128) + k4 * 4
                                    op(ACT, lambda kk=kk, t=t: nc.scalar.copy(out=hmT[:, kk:kk + 4, t * 128:(t + 1) * 128], in_=ptb[:, 0:512].rearrange("p (j t) -> p j t", t=128)),
                                       reads=[pb], writes=[B_hm[t]])
                        op(DVE, lambda: nc.vector.tensor_tensor(out=small[:, 0:KC], in0=ys_f[:], in1=ys_f[:], op=ALU.mult), reads=[B_ys], writes=[B_small])
                        op(DVE, lambda: nc.vector.reduce_sum(out=small[:, 32:33], in_=small[:, 0:KC], axis=AX.X), reads=[B_small], writes=[B_small])
                        pt, pb = psA.next()
                        op(PE, lambda: nc.tensor.matmul(pt[:, 0:1], lhsT=ones_f[:], rhs=small[:, 32:33], start=True, stop=True), reads=[B_small, B_const], writes=[pb])
                        op(DVE, lambda: nc.vector.tensor_scalar(out=small[:, 33:34], in0=pt[:, 0:1], scalar1=1.0 / D, scalar2=EPS, op0=ALU.mult, op1=ALU.add), reads=[pb], writes=[B_small])
                        op(ACT, lambda: nc.scalar.activation(out=small[:, 34:35], in_=small[:, 33:34], func=AF.Sqrt), reads=[B_small], writes=[B_small])
                        op(DVE, lambda: nc.vector.reciprocal(small[:, 35:36], small[:, 34:35]), reads=[B_small], writes=[B_small])
                        op(DVE, lambda: nc.vector.scalar_tensor_tensor(out=small[:, 0:KC], in0=ys_f[:], scalar=small[:, 35:36], in1=gpostT, op0=ALU.mult, op1=ALU.mult),
                           reads=[B_ys, B_small, B_vec], writes=[B_small])
                        op(DVE, lambda: nc.vector.tensor_tensor(out=hs[:], in0=hs[:], in1=small[:, 0:KC], op=ALU.add), reads=[B_hs, B_small], writes=[B_hs])
                        op(DVE, lambda: nc.vector.tensor_copy(xs_b[:], hs[:]), reads=[B_hs], writes=[B_xs])
                        barrier()
                        ck_auto()

                    with ExitStack() as ph6:
                        pT = sb(ph6, "pT", [128, 2, S], BF16)
                        BpT = Buf()
                        wpl = sb(ph6, "wpl", [128, 2, D], BF16)
                        Bwpl = Buf()
                        s_wpl = s_misc2
                        for _f in engs:
                            if _f is not PL:
                                PL.wait(_f.sem, _f.count)
                        for _s in all_sems:
                            PL.wait(_s, _s.issued)
                        _deps(PL, (), [Bwpl])
                        PL.wait(s_wpl, s_wpl.issued)
                        _wv = w_ple[l].rearrange("(k p) c -> p k c", p=128)
                        for _c8 in range(D // PW):
                            nc.gpsimd.dma_start(out=wpl[:, :, _c8 * PW:(_c8 + 1) * PW], in_=_wv[:, :, _c8 * PW:(_c8 + 1) * PW]).then_inc(s_wpl.h, 16)
                            s_wpl.issued += 16
                        Bwpl.writers = {s_wpl: s_wpl.issued}
                        sse = sb(ph6, "sse", [128, NT, 8], F32)
                        rse = sb(ph6, "rse", [128, NT], F32)
                        Bsse, Brse = Buf(), Buf()
                        op(DVE, lambda: nc.vector.memset(sse[:], 0.0), writes=[Bsse])
                        pin = [(sb(ph6, f"pin{i}", [128, PLE], F32), Buf(), s_ld[i]) for i in range(1)]
                        pinr = Ring(pin)
                        pbf = [(sb(ph6, f"pbf{i}", [128, PLE], BF16), Buf()) for i in range(2)]
                        pbfr = Ring(pbf)
                        junk = sb(ph6, "junk6", [128, 512], BF16)
                        Bjunk = Buf()
                        for t in range(NT):
                            pi_, pib_, pis_ = pinr.next()
                            dma(SP, pis_, pi_[:], pp_in[l, t * 128:(t + 1) * 128, :], writes=[pib_])
                            pb_, pbb_ = pbfr.next()
                            op(ACT, lambda: nc.scalar.copy(out=pb_[:], in_=pi_[:]), reads=[pib_], writes=[pbb_])
                            pt, pb = psA.next()
                            ptb = pt[:].bitcast(BF16)
                            for j in range(2):
                                op(PE, lambda j=j: nc.tensor.transpose(ptb[:, j * 128:(j + 1) * 128], pb_[:, j * 128:(j + 1) * 128], ident_b[:]),
                                   reads=[pbb_, B_const], writes=[pb], signal=(j == 1))
                            op(DVE, lambda t=t: nc.vector.tensor_copy(pT[:, :, t * 128:(t + 1) * 128], ptb[:, 0:256].rearrange("p (j t) -> p j t", t=128)),
                               reads=[pb], writes=[BpT])
                        with nc.allow_non_contiguous_dma(reason="tiny feature-major vector loads"):
                            dma(SP, s_misc, small[:, 40:42], ps_in[l].rearrange("(k p) -> p k", p=128), writes=[B_small])
                        op(DVE, lambda: nc.vector.tensor_copy(pss_b[:], small[:, 40:42]), reads=[B_small], writes=[B_pss])
                        for t in range(NT):
                            for cg in range(8):
                                pt, pb = psA.next()
                                for j in range(2):
                                    op(PE, lambda j=j: nc.tensor.matmul(pt[:], lhsT=pT[:, j, t * 128:(t + 1) * 128], rhs=wpl[:, j, cg * 512:(cg + 1) * 512],
                                                                        start=(j == 0), stop=(j == 1)), reads=[BpT, Bwpl], writes=[pb], signal=(j == 1))
                                op(ACT, lambda t=t, cg=cg: nc.scalar.activation(out=junk[:], in_=pt[:], func=AF.Square, accum_out=sse[:, t, cg:cg + 1]),
                                   reads=[pb], writes=[Bjunk, Bsse])
                        op(DVE, lambda: nc.vector.reduce_sum(out=rse[:], in_=sse[:], axis=AX.X), reads=[Bsse], writes=[Brse])
                        op(DVE, lambda: nc.vector.tensor_scalar(out=rse[:], in0=rse[:], scalar1=1.0 / D, scalar2=EPS, op0=ALU.mult, op1=ALU.add), reads=[Brse], writes=[Brse])
                        op(ACT, lambda: nc.scalar.activation(out=rse[:], in_=rse[:], func=AF.Sqrt), reads=[Brse], writes=[Brse])
                        op(DVE, lambda: nc.vector.reciprocal(rse[:], rse[:]), reads=[Brse], writes=[Brse])
                        if _dbg:
                            dma(SP, s_misc, dbg_pT[:, :], pT[:].rearrange("p a b -> p (a b)"), reads=[BpT])
                            dma(SP, s_misc, dbg_wpl[:, :], wpl[:].rearrange("p a b -> p (a b)"), reads=[Bwpl])
                            dma(SP, s_misc, dbg_ssq[:, 0:128], sse[:].rearrange("p a b -> p (a b)"), reads=[Bsse])
                            dma(SP, s_misc, dbg_rsy2[:, :], rse[:], reads=[Brse])
                        for kq in range(8):
                            pt, pb = psB.next()
                            for j4 in range(4):
                                kk = kq * 4 + j4
                                for j in range(2):
                                    op(PE, lambda j=j, kk=kk, j4=j4: nc.tensor.matmul(pt[:, j4:j4 + 1], lhsT=wpl[:, j, kk * 128:(kk + 1) * 128], rhs=pss_b[:, j:j + 1],
                                                                                      start=(j == 0), stop=(j == 1), skip_group_check=True),
                                       reads=[Bwpl, B_pss], writes=[pb], signal=(j == 1))
                            op(DVE, lambda kq=kq: nc.vector.tensor_copy(es_f[:, kq * 4:kq * 4 + 4], pt[:, 0:4]), reads=[pb], writes=[B_es])

                        bcb = [(sb(ph6, f"bcb{i}", [128, PW], F32), Buf(), s_ld[2]) for i in range(2)]
                        bcg = [(sb(ph6, f"bcg{i}", [128, PW], F32), Buf(), s_ld[3]) for i in range(2)]
                        hml = [(sb(ph6, f"hml{i}", [128, PW], F32), Buf(), s_ld[4 + i]) for i in range(2)]
                        hmlr = Ring(hml)
                        zt = [(sb(ph6, f"zt{i}", [128, PW], F32), Buf()) for i in range(2)]
                        et = [(sb(ph6, f"et{i}", [128, PW], F32), Buf()) for i in range(2)]
                        ot = [(sb(ph6, f"ot{i}", [128, PW], F32), Buf(), s_st[i]) for i in range(2)]
                        ztr, etr, otr = Ring(zt), Ring(et), Ring(ot)
                        wsrc = w_pg[l]
                        _hmq = []

                        def _issue_hm(pi_, t_):
                            h__, hb__, hs__ = hmlr.next()
                            dma(SP, hs__, h__[:], hmid[t_ * 128:(t_ + 1) * 128, pi_ * PW:(pi_ + 1) * PW], reads=[dr("hmid", t_, pi_ // 4)], writes=[hb__])
                            return h__, hb__

                        pend = [load_piece(wsrc, [(0, PW, 0)])]
                        for pi in range(NPC):
                            if pi + 1 < NPC:
                                pend.append(load_piece(wsrc, [((pi + 1) * PW, PW, 0)]))
                            wt, wb = pend.pop(0)
                            bb_, bbb_, bbs_ = bcb[pi % 2]
                            gg_, ggb_, ggs_ = bcg[pi % 2]
                            dma(SP, bbs_, bb_[:], bass.AP(tensor=b_pg.tensor, offset=b_pg[l].offset + pi * PW, ap=[[0, 128], [1, PW]]), writes=[bbb_])
                            dma(SP, ggs_, gg_[:], bass.AP(tensor=g_ple.tensor, offset=g_ple[l].offset + pi * PW, ap=[[0, 128], [1, PW]]), writes=[ggb_])
                            for t in range(NT):
                                ts = slice(t * 128, (t + 1) * 128)
                                cs = slice(pi * PW, (pi + 1) * PW)
                                if not _hmq:
                                    _hmq.append(_issue_hm(pi, t))
                                hm_, hmb_ = _hmq.pop(0)
                                pt, pb = psA.next()
                                for k in range(KC):
                                    op(PE, lambda k=k: nc.tensor.matmul(pt[:, 0:PW], lhsT=hmT[:, k, ts], rhs=wt[:, k, :], start=(k == 0), stop=(k == KC - 1)),
                                       reads=[wb, B_hm[t]], writes=[pb], signal=(k == KC - 1))
                                for j in range(2):
                                    op(PE, lambda j=j: nc.tensor.matmul(pt[:, PW:2 * PW], lhsT=pT[:, j, ts], rhs=wpl[:, j, cs], start=(j == 0), stop=(j == 1), skip_group_check=True),
                                       reads=[BpT, Bwpl], writes=[pb], signal=(j == 1))
                                _nt, _npi = (t + 1, pi) if t + 1 < NT else (0, pi + 1)
                                if _npi < NPC:
                                    _hmq.append(_issue_hm(_npi, _nt))
                                z_, zb_ = ztr.next()
                                e_, eb_ = etr.next()
                                o_, ob_, os_ = otr.next()
                                op(DVE, lambda: nc.vector.tensor_tensor(out=z_[:], in0=pt[:, 0:PW], in1=bb_[:], op=ALU.add), reads=[pb, bbb_], writes=[zb_])
                                op(ACT, lambda: nc.scalar.activation(out=z_[:], in_=z_[:], func=AF.Sigmoid), reads=[zb_], writes=[zb_])
                                op(DVE, lambda t=t: nc.vector.scalar_tensor_tensor(out=e_[:], in0=pt[:, PW:2 * PW], scalar=rse[:, t:t + 1], in1=gg_[:], op0=ALU.mult, op1=ALU.mult),
                                   reads=[pb, Brse, ggb_], writes=[eb_])
                                op(DVE, lambda: nc.vector.tensor_tensor(out=e_[:], in0=e_[:], in1=z_[:], op=ALU.mult), reads=[eb_, zb_], writes=[eb_])
                                op(DVE, lambda: nc.vector.tensor_tensor(out=o_[:], in0=e_[:], in1=hm_[:], op=ALU.add), reads=[eb_, hmb_], writes=[ob_])
                                dma(SP, os_, h_dst[ts, cs], o_[:], reads=[ob_], writes=[dr(hdkey, t)])
                            pt, pb = psB.next()
                            for j in range(2):
                                for k in range(KC):
                                    op(PE, lambda k=k, j=j: nc.tensor.matmul(pt[:, j:j + 1], lhsT=wt[:, k, j * 128:(j + 1) * 128], rhs=xs_b[:, k:k + 1],
                                                                             start=(k == 0), stop=(k == KC - 1), skip_group_check=True),
                                       reads=[wb, B_xs], writes=[pb], signal=(k == KC - 1))
                            op(DVE, lambda pi=pi: nc.vector.tensor_copy(ys_f[:, 2 * pi:2 * pi + 2], pt[:, 0:2]), reads=[pb], writes=[B_ys])
                        op(DVE, lambda: nc.vector.tensor_tensor(out=ys_f[:], in0=ys_f[:], in1=bpgT, op=ALU.add), reads=[B_ys, B_vec], writes=[B_ys])
                        op(ACT, lambda: nc.scalar.activation(out=ys_f[:], in_=ys_f[:], func=AF.Sigmoid), reads=[B_ys], writes=[B_ys])
                        op(DVE, lambda: nc.vector.tensor_tensor(out=small[:, 0:KC], in0=es_f[:], in1=es_f[:], op=ALU.mult), reads=[B_es], writes=[B_small])
                        op(DVE, lambda: nc.vector.reduce_sum(out=small[:, 32:33], in_=small[:, 0:KC], axis=AX.X), reads=[B_small], writes=[B_small])
                        pt, pb = psA.next()
                        op(PE, lambda: nc.tensor.matmul(pt[:, 0:1], lhsT=ones_f[:], rhs=small[:, 32:33], start=True, stop=True), reads=[B_small, B_const], writes=[pb])
                        op(DVE, lambda: nc.vector.tensor_scalar(out=small[:, 33:34], in0=pt[:, 0:1], scalar1=1.0 / D, scalar2=EPS, op0=ALU.mult, op1=ALU.add), reads=[pb], writes=[B_small])
                        op(ACT, lambda: nc.scalar.activation(out=small[:, 34:35], in_=small[:, 33:34], func=AF.Sqrt), reads=[B_small], writes=[B_small])
                        op(DVE, lambda: nc.vector.reciprocal(small[:, 35:36], small[:, 34:35]), reads=[B_small], writes=[B_small])
                        op(DVE, lambda: nc.vector.scalar_tensor_tensor(out=small[:, 0:KC], in0=es_f[:], scalar=small[:, 35:36], in1=gpleT, op0=ALU.mult, op1=ALU.mult),
                           reads=[B_es, B_small, B_vec], writes=[B_small])
                        op(DVE, lambda: nc.vector.tensor_tensor(out=small[:, 0:KC], in0=small[:, 0:KC], in1=ys_f[:], op=ALU.mult), reads=[B_small, B_ys], writes=[B_small])
                        op(DVE, lambda: nc.vector.tensor_tensor(out=hs[:], in0=hs[:], in1=small[:, 0:KC], op=ALU.add), reads=[B_hs, B_small], writes=[B_hs])
                        barrier()
                        ck_auto()
        except _Stop:
            pass
        _ST[0] = False

        hsr = sb(gs, "hsr", [KC, 128], F32)
        Bhsr = Buf()
        ptf_, pbf_ = psA.next()
        op(PE, lambda: nc.tensor.transpose(ptf_[0:KC, 0:128], hs[:], ident_f[:]), reads=[B_hs, B_const], writes=[pbf_])
        op(DVE, lambda: nc.vector.tensor_copy(hsr[:], ptf_[0:KC, 0:128]), reads=[pbf_], writes=[Bhsr])
        dma(SP, s_misc, y_s.rearrange("(k p) -> k p", p=128), hsr[:], reads=[Bhsr], writes=[dr("ys")])
        barrier(include_pool=True)
    return nc


_CACHE = {}


def kernel(x_prompt, x_sample, cache_k, cache_v, state_conv, p_prompt, p_sample, rel_bias,
           g_pre, w_in, conv_w, conv_b, ln_g, ln_b, w_out, g_post, w_ple, g_ple, w_pg, b_pg):
    f = lambda a: np.ascontiguousarray(np.asarray(a, dtype=np.float32))
    x_prompt, x_sample, cache_k, cache_v, state_conv = map(f, (x_prompt, x_sample, cache_k, cache_v, state_conv))
    p_prompt, p_sample = f(p_prompt), f(p_sample)
    shared = dict(rel_bias=f(rel_bias), g_pre=f(g_pre), w_in=f(w_in), conv_w=f(conv_w), conv_b=f(conv_b), ln_g=f(ln_g), ln_b=f(ln_b),
                  w_out=f(w_out), g_post=f(g_post), w_ple=f(w_ple), g_ple=f(g_ple), w_pg=f(w_pg), b_pg=f(b_pg))
    oh, ohs = _consts()
    shared["oh_c"] = oh.reshape(NB, 3 * TL)
    shared["ohs_c"] = ohs.reshape(NB, 3 * 128)
    shared["ident_c"] = np.eye(128, dtype=np.float32)
    if "nc" not in _CACHE:
        _CACHE["nc"] = build_nc()
    nc = _CACHE["nc"]
    in_maps = []
    for c in range(8):
        b = c % 4
        m = dict(shared)
        m["x_p"] = x_prompt[b]
        m["x_s"] = x_sample[c, 0]
        m["ck"] = np.ascontiguousarray(cache_k[:, c].reshape(DEPTH, LW, ATT))
        m["cv"] = np.ascontiguousarray(cache_v[:, c].reshape(DEPTH, LW, ATT))
        m["sc"] = np.ascontiguousarray(state_conv[:, c])
        m["pp"] = np.ascontiguousarray(p_prompt[:, b])
        m["psm"] = np.ascontiguousarray(p_sample[:, c, 0])
        in_maps.append(m)
    res = run_bass_kernel_spmd(nc, in_maps, core_ids=list(range(8)))
    R = res.results
    y_prompt = np.stack([R[b]["y_p"] for b in range(4)], 0)
    y_sample = np.stack([R[c]["y_s"] for c in range(8)], 0).reshape(8, 1, D)
    nk_p = np.stack([R[b]["nk_p"] for b in range(4)], 1).reshape(DEPTH, 4, S, H, HD)
    nv_p = np.stack([R[b]["nv_p"] for b in range(4)], 1).reshape(DEPTH, 4, S, H, HD)
    nc_p = np.stack([R[b]["nc_p"] for b in range(4)], 1)
    nk_s = np.stack([R[c]["nk_s"] for c in range(8)], 1).reshape(DEPTH, 8, LW, H, HD)
    nv_s = np.stack([R[c]["nv_s"] for c in range(8)], 1).reshape(DEPTH, 8, LW, H, HD)
    nc_s = np.stack([R[c]["nc_s"] for c in range(8)], 1)
    return (y_prompt, y_sample, nk_p, nv_p, nc_p, nk_s, nv_s, nc_s)
```
